# Optimizing a Trainium2 kernel written in Bass

```python
import math
import jax, jax.numpy as jnp
from jax import lax
import numpy as np

D_MODEL = 2048
BATCH = 4
SEQ = 2048
DEPTH = 2
DEC_BATCH = 32
DEC_SEQ = 8
PAST_LEN = 8192
PAGE_SIZE = 128

C_A = D_MODEL // 2
CONV_W = 31
C_B = D_MODEL // 2
DH_B = 64
H_B = C_B // (2 * DH_B)
C_C = D_MODEL // 2
DH_C = 128
H_C = C_C // DH_C
C_WINDOWS = (128, 512, 2048)
C_DILATIONS = (1, 4, 16)
N_GROUPS_C = len(C_WINDOWS)
IN_AB = 3 * C_A + 4 * C_B
IN_C = 3 * N_GROUPS_C * C_C + C_C
N_AB_LAYERS = (DEPTH + 1) // 2
N_C_LAYERS = DEPTH // 2
ALPHA = (2 * DEPTH) ** 0.25
BETA = (8 * DEPTH) ** -0.25
ROPE_THETA = 10000.0
Q_BLOCK = 128
LN_EPS = 1e-5

kernel_name = "hybrid_conv_diffattn_dilated_decode_step"


def layer_norm(x, g, b):
    xf = x.astype(jnp.float32)
    mu = jnp.mean(xf, -1, keepdims=True)
    var = jnp.mean(jnp.square(xf - mu), -1, keepdims=True)
    y = (xf - mu) * lax.rsqrt(var + LN_EPS) * g.astype(jnp.float32) + b.astype(jnp.float32)
    return y.astype(x.dtype)


def rms_norm(x, g):
    xf = x.astype(jnp.float32)
    ms = jnp.mean(jnp.square(xf), -1, keepdims=True)
    return (xf * lax.rsqrt(ms + LN_EPS) * g.astype(jnp.float32)).astype(x.dtype)


def rope(x, pos):
    dh = x.shape[-1]
    inv_freq = ROPE_THETA ** (-jnp.arange(0, dh, 2, dtype=jnp.float32) / dh)
    ang = pos.astype(jnp.float32)[:, None] * inv_freq[None, :]
    ang = jnp.concatenate([ang, ang], -1)[None, :, None, :]
    x1, x2 = x[..., : dh // 2], x[..., dh // 2:]
    rot = jnp.concatenate([-x2, x1], -1)
    y = x.astype(jnp.float32) * jnp.cos(ang) + rot.astype(jnp.float32) * jnp.sin(ang)
    return y.astype(x.dtype)


def diff_lambda(lq1, lk1, lq2, lk2, lam_init):
    e1 = jnp.exp(jnp.sum(lq1.astype(jnp.float32) * lk1.astype(jnp.float32)))
    e2 = jnp.exp(jnp.sum(lq2.astype(jnp.float32) * lk2.astype(jnp.float32)))
    return e1 - e2 + lam_init


def ab_project(x, pos, w_in, b_in):
    B, T = x.shape[:2]
    h = jnp.einsum("btd,dc->btc", x, w_in) + b_in
    cuts = [C_A, 2 * C_A, 3 * C_A, 3 * C_A + C_B, 3 * C_A + 2 * C_B, 3 * C_A + 3 * C_B]
    a_val, a_gate, g_a, q, k, v, g_b = jnp.split(h, cuts, axis=-1)
    glu = a_val * jax.nn.sigmoid(a_gate)
    q = rope(q.reshape(B, T, 2 * H_B, DH_B), pos) * (DH_B ** -0.5)
    k = rope(k.reshape(B, T, 2 * H_B, DH_B), pos)
    v = v.reshape(B, T, H_B, 2 * DH_B)
    return glu, g_a, q, k, v, g_b


def conv_branch(glu, buf, w_dw, b_dw, g, b):
    xpad = jnp.concatenate([buf, glu], axis=1)
    y = lax.conv_general_dilated(xpad, w_dw[:, None, :], window_strides=(1,), padding="VALID",
                                 dimension_numbers=("NWC", "WIO", "NWC"),
                                 feature_group_count=C_A) + b_dw
    y = jax.nn.silu(layer_norm(y, g, b))
    return y, xpad[:, xpad.shape[1] - (CONV_W - 1):]


def diff_softmax_mix(q, k, v, mask, lam):
    s = jnp.einsum("bqhd,bkhd->bhqk", q, k).astype(jnp.float32)
    s = jnp.where(mask[None, None], s, -jnp.inf)
    p = jax.nn.softmax(s, axis=-1)
    b, h2, tq, tk = p.shape
    p = p.reshape(b, h2 // 2, 2, tq, tk)
    pd = p[:, :, 0] - lam * p[:, :, 1]
    return jnp.einsum("bhqk,bkhd->bqhd", pd.astype(v.dtype), v)


def diff_attn_prompt(q, k, v, lam):
    B, T = q.shape[:2]
    nb = T // Q_BLOCK
    qb = q.reshape(B, nb, Q_BLOCK, 2 * H_B, DH_B).swapaxes(0, 1)
    kpos = jnp.arange(T)

    def one_block(args):
        qi, i = args
        qpos = i * Q_BLOCK + jnp.arange(Q_BLOCK)
        return diff_softmax_mix(qi, k, v, kpos[None, :] <= qpos[:, None], lam)

    out = lax.map(one_block, (qb, jnp.arange(nb)))
    return out.swapaxes(0, 1).reshape(B, T, H_B, 2 * DH_B)


def diff_attn_sample(q, k, v, lam, cache_k, cache_v, page_table):
    T = q.shape[1]
    n_past = page_table.shape[1] * PAGE_SIZE
    mask = jnp.concatenate([jnp.ones((T, n_past), bool), jnp.tril(jnp.ones((T, T), bool))], axis=1)

    def one_seq(args):
        pt, qi, ki, vi = args
        kp = cache_k[pt].reshape(n_past, 2 * H_B, DH_B)
        vp = cache_v[pt].reshape(n_past, H_B, 2 * DH_B)
        kk = jnp.concatenate([kp, ki], axis=0)[None]
        vv = jnp.concatenate([vp, vi], axis=0)[None]
        return diff_softmax_mix(qi[None], kk, vv, mask, lam)[0]

    return lax.map(one_seq, (page_table, q, k, v))


def ab_output(conv_out, g_a, attn, g_b, lam_init, subln_g, w_out):
    B, T = conv_out.shape[:2]
    attn = (rms_norm(attn, subln_g) * (1.0 - lam_init)).reshape(B, T, C_B)
    mixed = jnp.concatenate([conv_out * jax.nn.silu(g_a), attn * jax.nn.silu(g_b)], axis=-1)
    return jnp.einsum("btc,cd->btd", mixed, w_out)


def c_project(x, pos, w_in, b_in):
    B, T = x.shape[:2]
    h = jnp.einsum("btd,dc->btc", x, w_in) + b_in
    parts = jnp.split(h, [C_C * i for i in range(1, 3 * N_GROUPS_C + 1)], axis=-1)
    groups = []
    for g in range(N_GROUPS_C):
        q, k, v = (p.reshape(B, T, H_C, DH_C) for p in parts[3 * g: 3 * g + 3])
        groups.append((rope(q, pos) * (DH_C ** -0.5), rope(k, pos), v))
    return groups, parts[-1]


def dilated_prompt(q, k, v, window, dil):
    B, T = q.shape[:2]
    n = window // dil
    lq = T // dil
    qb_len = math.gcd(Q_BLOCK, lq)
    nb = lq // qb_len
    pad = jnp.zeros((B, window, H_C, DH_C), k.dtype)

    def by_residue(a):
        return jnp.concatenate([pad, a], axis=1).reshape(B, n + lq, dil, H_C, DH_C).transpose(0, 2, 1, 3, 4)

    kr, vr = by_residue(k), by_residue(v)
    qr = q.reshape(B, lq, dil, H_C, DH_C).transpose(0, 2, 1, 3, 4).reshape(B, dil, nb, qb_len, H_C, DH_C)
    win = jnp.arange(nb)[:, None] * qb_len + jnp.arange(n + qb_len)[None, :]
    kw, vw = kr[:, :, win], vr[:, :, win]
    i = jnp.arange(qb_len)[:, None]
    j = jnp.arange(n + qb_len)[None, :]
    band = (j >= i) & (j <= i + n)
    mask = band[None] & (win >= n)[:, None, :]
    s = jnp.einsum("brnqhd,brnkhd->brnhqk", qr, kw).astype(jnp.float32)
    s = jnp.where(mask[None, None, :, None], s, -jnp.inf)
    lse = jax.nn.logsumexp(s, axis=-1)
    p = jnp.exp(s - lse[..., None])
    out = jnp.einsum("brnhqk,brnkhd->brnqhd", p.astype(vw.dtype), vw)
    out = out.reshape(B, dil, lq, H_C, DH_C).transpose(0, 2, 1, 3, 4).reshape(B, T, H_C, DH_C)
    lse = lse.transpose(0, 1, 2, 4, 3).reshape(B, dil, lq, H_C).transpose(0, 2, 1, 3).reshape(B, T, H_C)
    return out, lse


def dilated_sample(q, k, v, buf, window, dil):
    B, T = q.shape[:2]
    lb = buf.shape[1]
    n = window // dil
    kk = jnp.concatenate([buf[:, :, 0], k], axis=1)
    vv = jnp.concatenate([buf[:, :, 1], v], axis=1)
    idx = (lb + jnp.arange(T))[:, None] - dil * jnp.arange(n + 1)[None, :]
    valid = idx >= 0
    idx = jnp.maximum(idx, 0)
    kg, vg = kk[:, idx], vv[:, idx]
    s = jnp.einsum("bqhd,bqjhd->bhqj", q, kg).astype(jnp.float32)
    s = jnp.where(valid[None, None], s, -jnp.inf)
    lse = jax.nn.logsumexp(s, axis=-1)
    p = jnp.exp(s - lse[..., None])
    out = jnp.einsum("bhqj,bqjhd->bqhd", p.astype(vg.dtype), vg)
    new_buf = jnp.concatenate([buf, jnp.stack([k, v], axis=2)], axis=1)[:, T:]
    return out, lse.transpose(0, 2, 1), new_buf


def c_output(outs, lses, gate, w_out):
    B, T = gate.shape[:2]
    wts = jax.nn.softmax(jnp.stack(lses, axis=-1), axis=-1).astype(outs[0].dtype)
    o = jnp.einsum("bthdg,bthg->bthd", jnp.stack(outs, axis=-1), wts).reshape(B, T, C_C)
    return jnp.einsum("btc,cd->btd", o * jax.nn.silu(gate), w_out)


def setup_inputs(seed: int = 0) -> dict:
    key = jax.random.key(seed)
    keys = iter(jax.random.split(key, 32))

    def nrm(shape, scale):
        return jax.random.normal(next(keys), shape, jnp.float32) * scale

    n_pages = PAST_LEN // PAGE_SIZE
    n_used = DEC_BATCH * n_pages
    n_phys = (n_used * 5) // 4
    page_table = jax.random.permutation(next(keys), n_phys)[:n_used].reshape(DEC_BATCH, n_pages).astype(jnp.int32)
    return {
        "x_prompt": nrm((BATCH, SEQ, D_MODEL), 1.0),
        "x_sample": nrm((DEC_BATCH, DEC_SEQ, D_MODEL), 1.0),
        "cache_kb": nrm((N_AB_LAYERS, n_phys, PAGE_SIZE, 2 * H_B, DH_B), 1.0),
        "cache_vb": nrm((N_AB_LAYERS, n_phys, PAGE_SIZE, H_B, 2 * DH_B), 1.0),
        "state_conv": nrm((N_AB_LAYERS, DEC_BATCH, CONV_W - 1, C_A), 0.5),
        "state_kv_c0": nrm((N_C_LAYERS, DEC_BATCH, min(C_WINDOWS[0], PAST_LEN), 2, H_C, DH_C), 1.0),
        "state_kv_c1": nrm((N_C_LAYERS, DEC_BATCH, min(C_WINDOWS[1], PAST_LEN), 2, H_C, DH_C), 1.0),
        "state_kv_c2": nrm((N_C_LAYERS, DEC_BATCH, min(C_WINDOWS[2], PAST_LEN), 2, H_C, DH_C), 1.0),
        "page_table": page_table,
        "w_in_ab": nrm((N_AB_LAYERS, D_MODEL, IN_AB), D_MODEL ** -0.5),
        "b_in_ab": nrm((N_AB_LAYERS, IN_AB), 0.02),
        "w_dw": nrm((N_AB_LAYERS, CONV_W, C_A), CONV_W ** -0.5),
        "b_dw": nrm((N_AB_LAYERS, C_A), 0.02),
        "ln_a_g": 1.0 + nrm((N_AB_LAYERS, C_A), 0.05),
        "ln_a_b": nrm((N_AB_LAYERS, C_A), 0.02),
        "lam_q1": nrm((N_AB_LAYERS, DH_B), 0.1),
        "lam_k1": nrm((N_AB_LAYERS, DH_B), 0.1),
        "lam_q2": nrm((N_AB_LAYERS, DH_B), 0.1),
        "lam_k2": nrm((N_AB_LAYERS, DH_B), 0.1),
        "subln_g": 1.0 + nrm((N_AB_LAYERS, 2 * DH_B), 0.05),
        "w_out_ab": nrm((N_AB_LAYERS, C_A + C_B, D_MODEL), BETA * (C_A + C_B) ** -0.5),
        "w_in_c": nrm((N_C_LAYERS, D_MODEL, IN_C), D_MODEL ** -0.5),
        "b_in_c": nrm((N_C_LAYERS, IN_C), 0.02),
        "w_out_c": nrm((N_C_LAYERS, C_C, D_MODEL), BETA * C_C ** -0.5),
        "post_ln_g": 1.0 + nrm((DEPTH, D_MODEL), 0.05),
        "post_ln_b": nrm((DEPTH, D_MODEL), 0.02),
    }


def reference(x_prompt, x_sample, cache_kb, cache_vb, state_conv, state_kv_c0, state_kv_c1, state_kv_c2,
              page_table, w_in_ab, b_in_ab, w_dw, b_dw, ln_a_g, ln_a_b, lam_q1, lam_k1, lam_q2, lam_k2,
              subln_g, w_out_ab, w_in_c, b_in_c, w_out_c, post_ln_g, post_ln_b):
    bp, tp = x_prompt.shape[:2]
    past_len = page_table.shape[1] * PAGE_SIZE
    pos_p = jnp.arange(tp)
    pos_s = past_len + jnp.arange(x_sample.shape[1])
    states_c = (state_kv_c0, state_kv_c1, state_kv_c2)
    xp, xs = x_prompt, x_sample
    kb_p, vb_p, conv_p, kb_s, vb_s, conv_s = [], [], [], [], [], []
    c_p = [[] for _ in range(N_GROUPS_C)]
    c_s = [[] for _ in range(N_GROUPS_C)]
    for layer in range(DEPTH):
        i = layer // 2
        if layer % 2 == 0:
            lam_init = 0.8 - 0.6 * math.exp(-0.3 * layer)
            lam = diff_lambda(lam_q1[i], lam_k1[i], lam_q2[i], lam_k2[i], lam_init)
            glu, g_a, q, k, v, g_b = ab_project(xp, pos_p, w_in_ab[i], b_in_ab[i])
            zero_buf = jnp.zeros((bp, CONV_W - 1, C_A), glu.dtype)
            a_out, buf = conv_branch(glu, zero_buf, w_dw[i], b_dw[i], ln_a_g[i], ln_a_b[i])
            att = diff_attn_prompt(q, k, v, lam)
            fp = ab_output(a_out, g_a, att, g_b, lam_init, subln_g[i], w_out_ab[i])
            kb_p.append(k); vb_p.append(v); conv_p.append(buf)
            glu, g_a, q, k, v, g_b = ab_project(xs, pos_s, w_in_ab[i], b_in_ab[i])
            a_out, buf = conv_branch(glu, state_conv[i], w_dw[i], b_dw[i], ln_a_g[i], ln_a_b[i])
            att = diff_attn_sample(q, k, v, lam, cache_kb[i], cache_vb[i], page_table)
            fs = ab_output(a_out, g_a, att, g_b, lam_init, subln_g[i], w_out_ab[i])
            kb_s.append(k); vb_s.append(v); conv_s.append(buf)
        else:
            groups, gate = c_project(xp, pos_p, w_in_c[i], b_in_c[i])
            outs, lses = [], []
            for g in range(N_GROUPS_C):
                q, k, v = groups[g]
                o, l = dilated_prompt(q, k, v, C_WINDOWS[g], C_DILATIONS[g])
                outs.append(o); lses.append(l)
                c_p[g].append(jnp.stack([k, v], axis=2)[:, tp - min(C_WINDOWS[g], tp):])
            fp = c_output(outs, lses, gate, w_out_c[i])
            groups, gate = c_project(xs, pos_s, w_in_c[i], b_in_c[i])
            outs, lses = [], []
            for g in range(N_GROUPS_C):
                q, k, v = groups[g]
                o, l, nbuf = dilated_sample(q, k, v, states_c[g][i], C_WINDOWS[g], C_DILATIONS[g])
                outs.append(o); lses.append(l)
                c_s[g].append(nbuf)
            fs = c_output(outs, lses, gate, w_out_c[i])
        xp = layer_norm(ALPHA * xp + fp, post_ln_g[layer], post_ln_b[layer])
        xs = layer_norm(ALPHA * xs + fs, post_ln_g[layer], post_ln_b[layer])
    return (xp, xs,
            jnp.stack(kb_p), jnp.stack(vb_p), jnp.stack(conv_p),
            jnp.stack(kb_s), jnp.stack(vb_s), jnp.stack(conv_s),
            jnp.stack(c_p[0]), jnp.stack(c_p[1]), jnp.stack(c_p[2]),
            jnp.stack(c_s[0]), jnp.stack(c_s[1]), jnp.stack(c_s[2]))
```

```python
import contextlib
import math
import numpy as np
import concourse.bass as bass
import concourse.mybir as mybir
from concourse.bass_utils import run_bass_kernel_spmd

F32 = mybir.dt.float32
BF16 = mybir.dt.bfloat16
I32 = mybir.dt.int32
AF = mybir.ActivationFunctionType
ALU = mybir.AluOpType
AX = mybir.AxisListType

NCORES = 8
D = 2048
T = 2048
NT = 17
NTOK = 2080
NPAGES = 64
NPHYS = 2560
ALPHA = (2 * 2) ** 0.25
LN_EPS = 1e-5
LAM_INIT0 = 0.8 - 0.6 * math.exp(-0.3 * 0)
SC_B = 64 ** -0.5
SC_C = 128 ** -0.5
C_W = (128, 512, 2048)
C_D = (1, 4, 16)


def I(name, *a, **kw):
    return (name, a, kw)


def _call(E, ins):
    name, a, kw = ins
    return getattr(E, name)(*a, **kw)


class Prog:
    ENG = ('pe', 'act', 'dve', 'pool', 'sp')

    def __init__(self, nc, stack):
        self.nc = nc
        self.stack = stack
        self.q = {e: [] for e in self.ENG}
        self.psem = {e: stack.enter_context(nc.semaphore("prog_" + e)) for e in self.ENG}
        self.pcnt = {e: 0 for e in self.ENG}
        self.waited = {e: {} for e in self.ENG}
        self.dsems = {}
        self.lastw = {}
        self.readers = {}
        self.ninst = 0
        self.dead = False

    def wait(self, e, *toks):
        if self.dead:
            return
        for tok in toks:
            if tok is None:
                continue
            sem, val, src = tok
            if src == 'pe' and e == 'pe':
                continue
            key = id(sem)
            if self.waited[e].get(key, 0) >= val:
                continue
            self.waited[e][key] = val
            self.q[e].append(lambda E, sem=sem, val=val: E.wait_ge(sem, val))

    def _hazards(self, reads, writes):
        toks = []
        for k in reads:
            t = self.lastw.get(k)
            if t is not None:
                toks.append(t)
        for k in writes:
            t = self.lastw.get(k)
            if t is not None:
                toks.append(t)
            toks.extend(self.readers.get(k, ()))
        return toks

    def _update(self, tok, reads, writes):
        for k in writes:
            self.lastw[k] = tok
            self.readers[k] = []
        for k in reads:
            lst = self.readers.setdefault(k, [])
            lst[:] = [t for t in lst if t[0] is not tok[0]]
            lst.append(tok)

    def op(self, e, fns, reads=(), writes=(), extra=()):
        if self.dead:
            return None
        if isinstance(fns, tuple):
            fns = [fns]
        self.wait(e, *self._hazards(reads, writes), *extra)
        self.ninst += len(fns)
        for fn in fns[:-1]:
            self.q[e].append(lambda E, fn=fn: _call(E, fn))
        self.pcnt[e] += 1
        sem = self.psem[e]
        fn = fns[-1]
        self.q[e].append(lambda E, fn=fn, sem=sem: _call(E, fn).then_inc(sem, 1))
        tok = (sem, self.pcnt[e], e)
        self._update(tok, reads, writes)
        return tok

    def dma(self, e, fn, semkey, reads=(), writes=(), extra=()):
        if self.dead:
            return None
        self.wait(e, *self._hazards(reads, writes), *extra)
        self.ninst += 1
        if semkey not in self.dsems:
            self.dsems[semkey] = [self.stack.enter_context(self.nc.semaphore("d%d" % len(self.dsems))), 0]
        ds = self.dsems[semkey]
        ds[1] += 16
        s = ds[0]
        self.q[e].append(lambda E, fn=fn, s=s: _call(E, fn).then_inc(s, 16))
        tok = (s, ds[1], 'dma')
        self._update(tok, reads, writes)
        return tok

    def fence(self, prefixes):
        if self.dead:
            return
        toks = []
        for k, t in list(self.lastw.items()):
            if isinstance(k, tuple) and k[0] in prefixes:
                toks.append(t)
        for k, lst in list(self.readers.items()):
            if isinstance(k, tuple) and k[0] in prefixes:
                toks.extend(lst)
        for e in self.ENG:
            self.wait(e, *toks)

    def all_tokens(self):
        toks = []
        for e in self.ENG:
            if self.pcnt[e]:
                toks.append((self.psem[e], self.pcnt[e], e))
        for k, ds in self.dsems.items():
            toks.append((ds[0], ds[1], 'dma'))
        return toks

    def emit(self):
        nc = self.nc
        with nc.Block() as block:
            @block.tensor
            def _(E):
                for f in self.q['pe']:
                    f(E)

            @block.scalar
            def _(E):
                for f in self.q['act']:
                    f(E)

            @block.vector
            def _(E):
                for f in self.q['dve']:
                    f(E)

            @block.gpsimd
            def _(E):
                for f in self.q['pool']:
                    f(E)

            @block.sync
            def _(E):
                for f in self.q['sp']:
                    f(E)


def chunk_cols(ch):
    return (512 * ch, 512) if ch < 4 else (2048, 32)


def chunk_tiles(ch):
    return list(range(4 * ch, 4 * ch + 4)) if ch < 4 else [16]


def tile_rows(t):
    return 128 if t < 16 else 32


def build_program(ncores=NCORES, stop_after=None, skip_l0=False):
    nc = bass.Bass("TRN2", target_bir_lowering=False)

    def din(name, shape, dt=F32):
        return nc.dram_tensor(name, list(shape), dt, kind="ExternalInput").ap()

    def dout(name, shape, dt=F32):
        return nc.dram_tensor(name, list(shape), dt, kind="ExternalOutput").ap()

    xp = din("xp", [T, D])
    xs = din("xs", [32, D])
    xsa = din("xsa", [256, D])
    wsb = din("wsb", [128, 16, 512])
    bsb_tm = din("bsb_tm", [1, 384])
    bsb_fm = din("bsb_fm", [128, 1])
    kbh = din("kbh", [NPHYS * 128, 128])
    vbh = din("vbh", [NPHYS * 128, 128])
    pt_in = din("pt", [1, 32 * NPAGES], I32)
    rope_s = din("rope_s", [8, 64])
    sel_in = din("sel", [128, 64])
    c12s_in = din("c12s", [16, 16])
    smask8_in = din("smask8", [16, 8])
    sconvT = din("sconvT", [4, 1024, 30])
    skv = [din("skv%d" % g, [4, C_W[g], 2, 1024]) for g in range(3)]
    w0 = din("w0", [28, 128, 16, 256])
    bfm0 = din("bfm0", [128, 28 * 2])
    btm0 = din("btm0", [28, 256])
    wo0 = din("wo0", [128, 16, 2048])
    w1 = din("w1", [40, 128, 16, 256])
    bfm1 = din("bfm1", [128, 40 * 2])
    btm1 = din("btm1", [40, 256])
    wo1 = din("wo1", [128, 8, 2048])
    wdw_in = din("wdw", [128, 8 * 31])
    cpar_in = din("cpar", [128, 24])
    lamv_in = din("lamv", [1, 256])
    subg_in = din("subg", [128, 1])
    pln_in = din("pln", [4, 2048])
    rope0 = din("rope0", [128, 17 * 64])
    rope1 = din("rope1", [128, 17 * 128])
    maskul_in = din("maskul", [128, 256])
    ident_in = din("ident", [128, 128])
    cmask_in = din("cmask", [128, 13 * 8])
    cmaskn_in = din("cmaskn", [32, 3 * 4 * 8])

    y_p = dout("y_p", [T, D])
    y_s = dout("y_s", [32, D])
    kb_p = dout("kb_p", [4, T, 256])
    vb_p = dout("vb_p", [4, T, 256])
    conv_pT = dout("conv_pT", [1024, 30])
    kb_sh = dout("kb_sh", [256, 128])
    vb_sh = dout("vb_sh", [256, 128])
    conv_sT = dout("conv_sT", [4, 1024, 30])
    kvc_p = [dout("kvc%d_p" % g, [2, 8, min(C_W[g], T), 128]) for g in range(3)]
    kvc_s = [dout("kvc%d_s" % g, [4, C_W[g], 2, 1024]) for g in range(3)]
    x1d = nc.dram_tensor("x1d", [NTOK, D], F32).ap()
    ccsrc_t = nc.dram_tensor("ccsrc", [256, 128], F32)
    ccdst_t = nc.dram_tensor("ccdst", [2048, 128], F32)
    ccsrc = ccsrc_t.ap()
    ccdst = ccdst_t.ap()

    with contextlib.ExitStack() as st:
        P = Prog(nc, st)

        def sb(name, shape, dt):
            return st.enter_context(nc.sbuf_tensor("s_" + name, list(shape), dt))

        bufA = sb("bufA", [128, 16 * NTOK], BF16)
        bufB = sb("bufB", [128, 16 * NTOK], BF16)
        slabs = [sb("slab%d" % i, [128, 16, 256], BF16) for i in range(3)]
        R3 = sb("R3", [128, 12288], BF16)
        R5 = sb("R5", [128, 6144], BF16)
        identb = sb("identb", [128, 128], BF16)
        identf = sb("identf", [128, 128], F32)
        onesb = sb("onesb", [128, 128], BF16)
        maskul = sb("maskulb", [128, 256], BF16)
        ropeT = [sb("ropeT%d" % i, [128, 128], F32) for i in range(2)]
        bfm = sb("bfm", [128, 80], F32)
        btm = [sb("btm%d" % i, [128, 256], F32) for i in range(3)]
        wdw = sb("wdwsb", [128, 8 * 31], F32)
        cpar = sb("cparsb", [128, 24], F32)
        lsm = sb("lsm", [128, 16], F32)
        bsb_fm_sb = sb("bsbfm", [128, 1], F32)
        cmask = sb("cmasksb", [128, 13 * 8], BF16)
        cmaskn = sb("cmasknsb", [32, 96], BF16)
        iot = sb("iot", [128, 1], I32)
        qTs = sb("qTs", [128, 24 * 32], BF16)
        kTs = sb("kTs", [128, 24 * 32], BF16)
        gTs = sb("gTs", [128, 8 * 32], BF16)
        stat = sb("stat", [128, 32], F32)

        psb = [st.enter_context(nc.psum_tensor("ps%d" % i, [128, 512], F32)) for i in range(8)]

        def PS(b):
            return psb[b]

        def PSB(b):
            return psb[b][:].bitcast(BF16)

        def pk(b):
            return ('ps', b)

        XTA = bufA[:].rearrange("p (k t) -> p k t", k=16)
        MIX = bufB[:].rearrange("p (k t) -> p k t", k=16)
        WO0 = bufA[:, 0:16 * 2048].rearrange("p (k t) -> p k t", k=16)
        MIX1 = bufA[:, 0:8 * NTOK].rearrange("p (k t) -> p k t", k=8)
        R7 = bufA[:, 8 * NTOK:16 * NTOK]
        WO1 = R7[:, 0:8 * 2048].rearrange("p (k t) -> p k t", k=8)
        MSTAT = bufB[:, 8 * NTOK:16 * NTOK].bitcast(F32)[:, 0:2 * NTOK].rearrange("p (a t) -> p a t", a=2)
        R3f = R3[:].bitcast(F32)
        R5f = R5[:].bitcast(F32)
        lamt = R5f[:, 256:512]
        cstp = R5f[:, 2048:2288]
        csts = R5f[:, 2304:2560]

        def mixkeys(cs, ch):
            return [('mix', c, t) for c in cs for t in chunk_tiles(ch)]

        final_toks = []

        def _phase_end(name):
            if stop_after == name:
                P.dead = True

        def ld(e, dst, src, key):
            return P.dma(e, I('dma_start', out=dst, in_=src), ('c', key), writes=[('c', key)])

        ld('pool', identb[:], ident_in, 'identb')
        ld('sp', identf[:], ident_in, 'identf')
        ld('pool', maskul[:], maskul_in, 'maskul')
        ld('sp', bfm[:, 0:56], bfm0, 'bfm')
        ld('sp', wdw[:], wdw_in, 'wdw')
        ld('sp', cpar[:], cpar_in, 'cpar')
        ld('sp', lamt, lamv_in.partition_broadcast(128), 'lamt')
        ld('sp', lsm[:, 7:8], subg_in, 'subg')
        ld('sp', bsb_fm_sb[:], bsb_fm, 'bsbfm')
        ld('pool', cmask[:], cmask_in, 'cmask')
        ld('pool', cmaskn[:], cmaskn_in, 'cmaskn')
        P.op('pool', I('iota', iot[:], pattern=[[0, 1]], base=0, channel_multiplier=1), writes=[('c', 'iot')])
        P.op('dve', I('memset', onesb[:], 1.0), writes=[('c', 'ones')])
        lt = lamt.rearrange("p (a b) -> p a b", a=4)
        tmpl = R5f[:, 0:64]
        P.op('dve', I('tensor_tensor', out=tmpl, in0=lt[:, 0, :], in1=lt[:, 1, :], op=ALU.mult),
             reads=[('c', 'lamt')], writes=[('R5', 'l')])
        P.op('dve', I('reduce_sum', out=lsm[:, 0:1], in_=tmpl, axis=AX.X), reads=[('R5', 'l')], writes=[('c', 'l0')])
        P.op('dve', I('tensor_tensor', out=tmpl, in0=lt[:, 2, :], in1=lt[:, 3, :], op=ALU.mult),
             reads=[('c', 'lamt')], writes=[('R5', 'l')])
        P.op('dve', I('reduce_sum', out=lsm[:, 1:2], in_=tmpl, axis=AX.X), reads=[('R5', 'l')], writes=[('c', 'l1')])
        P.op('act', I('activation', out=lsm[:, 2:4], in_=lsm[:, 0:2], func=AF.Exp),
             reads=[('c', 'l0'), ('c', 'l1')], writes=[('c', 'l2')])
        P.op('dve', I('tensor_tensor', out=lsm[:, 4:5], in0=lsm[:, 2:3], in1=lsm[:, 3:4], op=ALU.subtract),
             reads=[('c', 'l2')], writes=[('c', 'l4')])
        P.op('dve', I('tensor_scalar', out=lsm[:, 5:6], in0=lsm[:, 4:5], scalar1=LAM_INIT0, scalar2=-1.0,
                                              op0=ALU.add, op1=ALU.mult),
             reads=[('c', 'l4')], writes=[('c', 'neglam')])
        P.op('dve', I('tensor_single_scalar', out=lsm[:, 6:7], in_=lsm[:, 7:8], scalar=1.0 - LAM_INIT0, op=ALU.mult),
             reads=[('c', 'subg')], writes=[('c', 'subgs')])
        CK = [('c', k) for k in ('identb', 'identf', 'maskul', 'ones', 'neglam', 'subgs', 'cmt', 'selB', 'smaskB',
                                 'cmask', 'cmaskn', 'cpar', 'wdw')]

        for g in range(3):
            Lb = C_W[g]
            n = (Lb - 8) * 2048
            for s in range(4):
                src = skv[g][s, 8:Lb].rearrange("a b c -> (a b c)").rearrange("(p x) -> p x", p=128)
                dst = kvc_s[g][s, 0:Lb - 8].rearrange("a b c -> (a b c)").rearrange("(p x) -> p x", p=128)
                final_toks.append(P.dma('sp', I('dma_start', out=dst, in_=src), ('d2d',)))
        final_toks.append(P.dma('sp', I('dma_start', out=conv_sT[:, :, 0:22], in_=sconvT[:, :, 8:30]), ('d2d',)))

        class Slabs:
            def __init__(self, w, btm_d, n, name, tm_set):
                self.w = w; self.btm_d = btm_d; self.n = n; self.name = name; self.issued = 0; self.tm = tm_set

            def key(self, i):
                return ('slab', i % 3)

            def pref(self, upto):
                while self.issued <= min(upto, self.n - 1):
                    i = self.issued
                    slot = i % 3
                    P.dma('pool', I('dma_start', out=slabs[slot][:], in_=self.w[i]),
                          ('slab', slot), writes=[('slab', slot)])
                    if i in self.tm:
                        P.dma('sp', I('dma_start', out=btm[slot][:],
                                                                           in_=self.btm_d[i:i + 1, :].partition_broadcast(128)),
                              ('btm', slot), writes=[('btm', slot)])
                    self.issued += 1

            def get(self, i):
                self.pref(i + 2)
                return slabs[i % 3], btm[i % 3], ('slab', i % 3), ('btm', i % 3)

        def proj_fm(XT, xkey, slab, skey, half, ch, bank):
            c0, n = chunk_cols(ch)
            fns = [I('matmul', PS(bank)[:, 0:n], lhsT=slab[:, kc, half * 128:(half + 1) * 128],
                                             rhs=XT[:, kc, c0:c0 + n], start=(kc == 0), stop=(kc == 15))
                   for kc in range(16)]
            P.op('pe', fns, reads=[skey] + [(xkey, t) for t in chunk_tiles(ch)], writes=[pk(bank)])
            return n

        def proj_tm(XT, xkey, slab, skey, tok_ap_fn, rows, bank, c0, ncols, xtiles):
            fns = [I('matmul', PS(bank)[0:rows, 0:ncols], lhsT=tok_ap_fn(kc),
                                             rhs=slab[:, kc, c0:c0 + ncols], start=(kc == 0), stop=(kc == 15))
                   for kc in range(16)]
            P.op('pe', fns, reads=[skey] + [(xkey, t) for t in xtiles], writes=[pk(bank)])

        def rope_ops(src3, dst3, rows, nh, half, cosv, sinv, tkey_r, tkey_w, tmpbase, ropekey):
            cb = cosv.unsqueeze(1).to_broadcast([rows, nh, half])
            sn = sinv.unsqueeze(1).to_broadcast([rows, nh, half])
            x1 = src3[:, :, 0:half]
            x2 = src3[:, :, half:2 * half]
            n = nh * half
            t1 = R5f[0:rows, tmpbase:tmpbase + n].rearrange("p (a b) -> p a b", a=nh)
            t2 = R5f[0:rows, tmpbase + n:tmpbase + 2 * n].rearrange("p (a b) -> p a b", a=nh)
            t3 = R5f[0:rows, tmpbase + 2 * n:tmpbase + 3 * n].rearrange("p (a b) -> p a b", a=nh)
            t4 = R5f[0:rows, tmpbase + 3 * n:tmpbase + 4 * n].rearrange("p (a b) -> p a b", a=nh)
            rk = [tkey_r, ropekey]
            P.op('dve', I('tensor_tensor', out=t1, in0=x1, in1=cb, op=ALU.mult), reads=rk, writes=[('R5', 'rt1')])
            P.op('dve', I('tensor_tensor', out=t2, in0=x2, in1=sn, op=ALU.mult), reads=rk, writes=[('R5', 'rt2')])
            P.op('pool', I('tensor_tensor', out=t3, in0=x2, in1=cb, op=ALU.mult), reads=rk, writes=[('R5', 'rt3')])
            P.op('pool', I('tensor_tensor', out=t4, in0=x1, in1=sn, op=ALU.mult), reads=rk, writes=[('R5', 'rt4')])
            P.op('dve', I('tensor_tensor', out=dst3[:, :, 0:half], in0=t1, in1=t2, op=ALU.subtract),
                 reads=[('R5', 'rt1'), ('R5', 'rt2')], writes=[tkey_w + ('a',)])
            P.op('pool', I('tensor_tensor', out=dst3[:, :, half:2 * half], in0=t3, in1=t4, op=ALU.add),
                 reads=[('R5', 'rt3'), ('R5', 'rt4')], writes=[tkey_w + ('b',)])
            return [tkey_w + ('a',), tkey_w + ('b',)]

        def outproj_layer(layer, MIXV, ncg, WOV, wkey, xsrc_fn, emit_x1):
            gB = R3f[:, 0:2048]
            bB = R3f[:, 2048:4096]
            yst = R3f[:, 4096:6144]
            xst = [R5f[:, 0:512], R5f[:, 512:1024]]
            ybf = R5[:, 2048:4096]
            junk = R5[:, 4096:4608]
            P.dma('sp', I('dma_start', out=gB, in_=pln_in[2 * layer:2 * layer + 1, :].partition_broadcast(128)),
                  ('R3', 'gB'), writes=[('R3', 'gB')])
            P.dma('sp', I('dma_start', out=bB, in_=pln_in[2 * layer + 1:2 * layer + 2, :].partition_broadcast(128)),
                  ('R3', 'bB'), writes=[('R3', 'bB')])
            for t in range(NT):
                rows = tile_rows(t)
                bset = 4 * (t % 2)
                mk = [('mix', c, t) for c in range(ncg)] if layer == 0 else [('mix1', t)]
                for nb in range(4):
                    fns = [I('matmul', PS(bset + nb)[0:rows, :], lhsT=MIXV[:, cc, t * 128:t * 128 + rows],
                                                            rhs=WOV[:, cc, nb * 512:(nb + 1) * 512],
                                                            start=(cc == 0), stop=(cc == ncg - 1)) for cc in range(ncg)]
                    P.op('pe', fns, reads=mk + [wkey], writes=[pk(bset + nb)])
                for nb in range(4):
                    xs_ = xst[nb % 2]
                    src = xsrc_fn(t, rows, nb)
                    P.dma('sp', I('dma_start', out=xs_[0:rows, :], in_=src),
                          ('R5', 'xst', nb % 2), writes=[('R5', 'xst', nb % 2)])
                    P.op('dve', I('scalar_tensor_tensor',
                        out=PS(bset + nb)[0:rows, :], in0=xs_[0:rows, :], scalar=ALPHA, in1=PS(bset + nb)[0:rows, :],
                        op0=ALU.mult, op1=ALU.add, accum_out=stat[0:rows, nb:nb + 1]),
                        reads=[('R5', 'xst', nb % 2)], writes=[pk(bset + nb), ('st', nb)])
                    P.op('act', I('activation', out=junk[0:rows, :], in_=PS(bset + nb)[0:rows, :], func=AF.Square,
                                                              accum_out=stat[0:rows, 4 + nb:5 + nb]),
                         reads=[pk(bset + nb)], writes=[('R5', 'junk'), ('st', 4 + nb)])
                sk = [('st', i) for i in range(8)]
                P.op('dve', I('reduce_sum', out=stat[0:rows, 8:9], in_=stat[0:rows, 0:4], axis=AX.X), reads=sk, writes=[('st', 8)])
                P.op('dve', I('reduce_sum', out=stat[0:rows, 9:10], in_=stat[0:rows, 4:8], axis=AX.X), reads=sk, writes=[('st', 9)])
                P.op('dve', I('tensor_single_scalar', out=stat[0:rows, 10:11], in_=stat[0:rows, 8:9], scalar=1.0 / D, op=ALU.mult), reads=[('st', 8)], writes=[('st', 10)])
                P.op('dve', I('tensor_tensor', out=stat[0:rows, 11:12], in0=stat[0:rows, 10:11], in1=stat[0:rows, 10:11],
                                                      op=ALU.mult), reads=[('st', 10)], writes=[('st', 11)])
                P.op('dve', I('scalar_tensor_tensor', out=stat[0:rows, 12:13], in0=stat[0:rows, 9:10], scalar=1.0 / D,
                                                             in1=stat[0:rows, 11:12], op0=ALU.mult, op1=ALU.subtract),
                     reads=[('st', 9), ('st', 11)], writes=[('st', 12)])
                P.op('act', I('activation', out=stat[0:rows, 13:14], in_=stat[0:rows, 12:13], func=AF.Sqrt, bias=LN_EPS),
                     reads=[('st', 12)], writes=[('st', 13)])
                P.op('dve', I('reciprocal', out=stat[0:rows, 14:15], in_=stat[0:rows, 13:14]), reads=[('st', 13)],
                     writes=[('st', 14)])
                for nb in range(4):
                    P.op('dve', I('tensor_scalar', out=yst[0:rows, nb * 512:(nb + 1) * 512], in0=PS(bset + nb)[0:rows, :],
                                                                 scalar1=stat[0:rows, 10:11], scalar2=stat[0:rows, 14:15],
                                                                 op0=ALU.subtract, op1=ALU.mult),
                         reads=[pk(bset + nb), ('st', 10), ('st', 14)], writes=[('R3', 'yst', nb)])
                ysk = [('R3', 'yst', nb) for nb in range(4)]
                P.op('dve', I('tensor_tensor', out=yst[0:rows, :], in0=yst[0:rows, :], in1=gB[0:rows, :], op=ALU.mult),
                     reads=ysk + [('R3', 'gB')], writes=ysk)
                P.op('pool', I('tensor_tensor', out=yst[0:rows, :], in0=yst[0:rows, :], in1=bB[0:rows, :], op=ALU.add),
                     reads=ysk + [('R3', 'bB')], writes=ysk)
                if emit_x1:
                    P.dma('sp', I('dma_start', out=x1d[t * 128:t * 128 + rows, :], in_=yst[0:rows, :]),
                          ('R3', 'ystd'), reads=ysk, writes=[('x1d', t)])
                    P.op('act', I('copy', out=ybf[0:rows, :], in_=yst[0:rows, :]), reads=ysk, writes=[('R5', 'ybf')])
                    for hf in range(2):
                        bank = bset + hf
                        pv = PSB(bank).rearrange("p (a b) -> p a b", a=8)
                        fns = [I('transpose', out=pv[:, kc % 8, 0:rows], in_=ybf[0:rows, kc * 128:(kc + 1) * 128],
                                                                   identity=identb[0:rows, 0:rows])
                               for kc in range(8 * hf, 8 * hf + 8)]
                        P.op('pe', fns, reads=[('R5', 'ybf'), ('c', 'identb')], writes=[pk(bank)])
                        eng = 'act' if hf == 0 else 'dve'
                        if eng == 'act':
                            P.op('act', I('copy', out=MIX[:, 8 * hf:8 * hf + 8, t * 128:t * 128 + rows],
                                                                        in_=pv[:, :, 0:rows]),
                                 reads=[pk(bank)], writes=[('mix', c, t) for c in range(8 * hf, 8 * hf + 8)] + [('x1T', t, hf)])
                        else:
                            P.op('dve', I('tensor_copy', out=MIX[:, 8 * hf:8 * hf + 8, t * 128:t * 128 + rows],
                                                                              in_=pv[:, :, 0:rows]),
                                 reads=[pk(bank)], writes=[('mix', c, t) for c in range(8 * hf, 8 * hf + 8)] + [('x1T', t, hf)])
                else:
                    dst = y_p[t * 128:t * 128 + rows, :] if t < 16 else y_s[:, :]
                    final_toks.append(P.dma('sp', I('dma_start', out=dst, in_=yst[0:rows, :]),
                                            ('R3', 'ystd'), reads=ysk))

        TM0 = set()
        for hp in range(4):
            TM0.update([12 + 4 * hp, 13 + 4 * hp, 14 + 4 * hp])
        S0 = Slabs(w0, btm0, 28, 'w0', TM0)
        S0.pref(2)

        _phase_end('setup')
        if skip_l0:
            P.dead = True
        xstg = [R3[:, 0:2048], R3[:, 2048:4096]]
        for t in range(NT):
            rows = tile_rows(t)
            slot = t % 2
            src = xp[t * 128:(t + 1) * 128, :] if t < 16 else xs[:, :]
            P.dma('pool', I('dma_start', out=xstg[slot][0:rows, :], in_=src),
                  ('R3', 'xstg', slot), writes=[('R3', 'xstg', slot)])
            for hf in range(2):
                bank = hf
                pv = PSB(bank).rearrange("p (a b) -> p a b", a=8)
                fns = [I('transpose', out=pv[:, kc % 8, 0:rows], in_=xstg[slot][0:rows, kc * 128:(kc + 1) * 128],
                                                           identity=identb[0:rows, 0:rows]) for kc in range(8 * hf, 8 * hf + 8)]
                P.op('pe', fns, reads=[('R3', 'xstg', slot), ('c', 'identb')], writes=[pk(bank)])
                if hf == 0:
                    P.op('act', I('copy', out=XTA[:, 0:8, t * 128:t * 128 + rows], in_=pv[:, :, 0:rows]),
                         reads=[pk(bank)], writes=[('xT', t, 0)])
                else:
                    P.op('dve', I('tensor_copy', out=XTA[:, 8:16, t * 128:t * 128 + rows], in_=pv[:, :, 0:rows]),
                         reads=[pk(bank)], writes=[('xT', t, 1)])
        for t in range(NT):
            P.lastw[('xT', t)] = P.lastw.get(('xT', t, 1))
        xt_extra = [P.lastw.get(('xT', t, 0)) for t in range(NT)]
        P.wait('pe', *xt_extra)
        P.fence(['R3'])

        _phase_end('X')
        GLU = R3[:, 0:2230]
        GLUS = R3[:, 2078:2230].rearrange("p (s j) -> p s j", j=38)
        DIAG = [R3[:, 2304:2304 + 31 * 128].rearrange("p (j c) -> p j c", j=31),
                R3[:, 6400:6400 + 31 * 128].rearrange("p (j c) -> p j c", j=31)]
        P.op('dve', I('memset', GLU[:, 0:30], 0.0), writes=[('R3', 'glu', 'pad')])
        sigt = [R5f[:, 0:512], R5f[:, 512:1024]]
        for c in range(8):
            slab, _, skey, _ = S0.get(c)
            P.dma('pool', I('dma_start', out=GLUS[:, :, 0:30],
                                                     in_=sconvT[:, c * 128:(c + 1) * 128, :].rearrange("s p j -> p s j")),
                  ('R3', 'glus'), writes=[('R3', 'glu', 4)])
            for ch in range(5):
                bV, bG = (0, 1) if ch % 2 == 0 else (2, 3)
                n = proj_fm(XTA, 'xT', slab, skey, 0, ch, bV)
                proj_fm(XTA, 'xT', slab, skey, 1, ch, bG)
                sg = sigt[ch % 2]
                P.op('act', I('activation', out=sg[:, 0:n], in_=PS(bG)[:, 0:n], func=AF.Sigmoid,
                                                                        bias=bfm[:, 2 * c + 1:2 * c + 2]),
                     reads=[pk(bG), ('c', 'bfm')], writes=[('R5', 'sig', ch % 2)])
                if ch < 4:
                    P.op('dve', I('scalar_tensor_tensor',
                        out=GLU[:, 30 + 512 * ch:30 + 512 * ch + 512], in0=PS(bV)[:, :], scalar=bfm[:, 2 * c:2 * c + 1], in1=sg[:, :],
                        op0=ALU.add, op1=ALU.mult),
                        reads=[pk(bV), ('R5', 'sig', ch % 2), ('c', 'bfm')], writes=[('R3', 'glu', ch)])
                    if ch == 3:
                        P.op('dve', I('scalar_tensor_tensor',
                            out=cstp[:, 30 * c:30 * c + 30], in0=PS(bV)[:, 482:512], scalar=bfm[:, 2 * c:2 * c + 1],
                            in1=sg[:, 482:512], op0=ALU.add, op1=ALU.mult),
                            reads=[pk(bV), ('R5', 'sig', ch % 2)], writes=[('R5', 'cstp', c)])
                else:
                    pv3 = PS(bV)[:, 0:32].rearrange("p (s t) -> p s t", t=8)
                    sg3 = sg[:, 0:32].rearrange("p (s t) -> p s t", t=8)
                    P.op('dve', I('scalar_tensor_tensor',
                        out=GLUS[:, :, 30:38], in0=pv3, scalar=bfm[:, 2 * c:2 * c + 1], in1=sg3, op0=ALU.add, op1=ALU.mult),
                        reads=[pk(bV), ('R5', 'sig', ch % 2), ('R3', 'glu', 4)], writes=[('R3', 'glu', 5)])
                    P.op('dve', I('scalar_tensor_tensor',
                        out=csts[:, 32 * c:32 * c + 32].rearrange("p (s t) -> p s t", t=8), in0=pv3,
                        scalar=bfm[:, 2 * c:2 * c + 1], in1=sg3, op0=ALU.add, op1=ALU.mult),
                        reads=[pk(bV), ('R5', 'sig', ch % 2)], writes=[('R5', 'csts', c)])
            dg = DIAG[c % 2]
            for j in range(31):
                P.op('pool', I('tensor_single_scalar', out=dg[:, j, :], in_=identb[:], scalar=wdw[:, 31 * c + j:31 * c + j + 1], op=ALU.mult),
                     reads=[('c', 'identb'), ('c', 'wdw')], writes=[('R3', 'diag', c % 2)])
            glk = [('R3', 'glu', i) for i in range(6)] + [('R3', 'glu', 'pad')]
            for ch in range(5):
                bank = 4 + ch % 2
                if ch < 4:
                    fns = [I('matmul', PS(bank)[:, :], lhsT=dg[:, j, :], rhs=GLU[:, 512 * ch + j:512 * ch + j + 512],
                                                                 start=(j == 0), stop=(j == 30)) for j in range(31)]
                    n = 512
                else:
                    fns = []
                    for s in range(4):
                        fns += [I('matmul', PS(bank)[:, 8 * s:8 * s + 8], lhsT=dg[:, j, :], rhs=GLUS[:, s, j:j + 8],
                                                                    start=(j == 0 and s == 0), stop=(j == 30)) for j in range(31)]
                    n = 32
                P.op('pe', fns, reads=glk + [('R3', 'diag', c % 2)], writes=[pk(bank)])
                c0, _ = chunk_cols(ch)
                P.op('act', I('activation', out=MIX[:, c, c0:c0 + n], in_=PS(bank)[:, 0:n],
                                                                            func=AF.Identity, bias=cpar[:, c:c + 1]),
                     reads=[pk(bank), ('c', 'cpar')], writes=mixkeys([c], ch))
        final_toks.append(P.dma('sp', I('dma_start', out=conv_pT.rearrange("(c p) j -> p c j", p=128),
                                                            in_=cstp.rearrange("p (c j) -> p c j", c=8)),
                                ('out', 'cst'), reads=[('R5', 'cstp', c) for c in range(8)]))
        for s in range(4):
            final_toks.append(P.dma('sp', I('dma_start',
                out=conv_sT[s, :, 22:30].rearrange("(c p) j -> p c j", p=128),
                in_=csts.rearrange("p (c s t) -> p c s t", c=8, s=4)[:, :, s, :]),
                ('out', 'cst'), reads=[('R5', 'csts', c) for c in range(8)]))

        _phase_end('conv')
        sqt = [R5[:, 2048:2560], R5[:, 2560:3072]]
        for ch in range(5):
            c0, n = chunk_cols(ch)
            fns = [I('matmul', PS(6)[:, 0:n], lhsT=onesb[:], rhs=MIX[:, c, c0:c0 + n], start=(c == 0), stop=(c == 7))
                   for c in range(8)]
            P.op('pe', fns, reads=mixkeys(range(8), ch) + [('c', 'ones')], writes=[pk(6)])
            for c in range(8):
                sq = sqt[c % 2]
                P.op('act', I('activation', out=sq[:, 0:n], in_=MIX[:, c, c0:c0 + n], func=AF.Square),
                     reads=mixkeys([c], ch), writes=[('R5', 'sq', c % 2)])
                P.op('pe', I('matmul', PS(7)[:, 0:n], lhsT=onesb[:], rhs=sq[:, 0:n], start=(c == 0), stop=(c == 7)),
                     reads=[('R5', 'sq', c % 2)], writes=[pk(7)])
            mean = MSTAT[:, 0, c0:c0 + n]
            rstd = MSTAT[:, 1, c0:c0 + n]
            t1 = R5f[:, 0:n]
            t2 = R5f[:, 512:512 + n]
            P.op('dve', I('tensor_single_scalar', out=mean, in_=PS(6)[:, 0:n], scalar=1.0 / 1024, op=ALU.mult),
                 reads=[pk(6)], writes=[('mst', 0, ch)])
            P.op('dve', I('tensor_tensor', out=t1, in0=mean, in1=mean, op=ALU.mult), reads=[('mst', 0, ch)],
                 writes=[('R5', 'sig', 0)])
            P.op('dve', I('scalar_tensor_tensor', out=t2, in0=PS(7)[:, 0:n], scalar=1.0 / 1024, in1=t1, op0=ALU.mult,
                                                         op1=ALU.subtract),
                 reads=[pk(7), ('R5', 'sig', 0)], writes=[('R5', 'sig', 1)])
            P.op('act', I('activation', out=t2, in_=t2, func=AF.Sqrt, bias=LN_EPS), reads=[('R5', 'sig', 1)],
                 writes=[('R5', 'sig', 1)])
            P.op('dve', I('reciprocal', out=rstd, in_=t2), reads=[('R5', 'sig', 1)], writes=[('mst', 1, ch)])

        _phase_end('lnstat')
        for i in range(4):
            slab, _, skey, _ = S0.get(8 + i)
            for half in range(2):
                c = 2 * i + half
                for ch in range(5):
                    c0, n = chunk_cols(ch)
                    bank = (2 * half + ch) % 4
                    proj_fm(XTA, 'xT', slab, skey, half, ch, bank)
                    sga = R5f[:, 0:n]
                    z = R5f[:, 512:512 + n]
                    a = R5f[:, 1024:1024 + n]
                    P.op('act', I('activation',
                        out=sga, in_=PS(bank)[:, 0:n], func=AF.Silu, bias=bfm[:, 2 * (8 + i) + half:2 * (8 + i) + half + 1]),
                        reads=[pk(bank), ('c', 'bfm')], writes=[('R5', 'sig', 0)])
                    P.op('dve', I('tensor_tensor', out=z, in0=MIX[:, c, c0:c0 + n],
                                                                               in1=MSTAT[:, 0, c0:c0 + n], op=ALU.subtract),
                         reads=mixkeys([c], ch) + [('mst', 0, ch)], writes=[('R5', 'sig', 1)])
                    P.op('dve', I('tensor_tensor', out=z, in0=z, in1=MSTAT[:, 1, c0:c0 + n], op=ALU.mult),
                         reads=[('R5', 'sig', 1), ('mst', 1, ch)], writes=[('R5', 'sig', 1)])
                    P.op('act', I('activation', out=a, in_=z, func=AF.Silu, scale=cpar[:, 8 + c:9 + c],
                                                                    bias=cpar[:, 16 + c:17 + c]),
                         reads=[('R5', 'sig', 1), ('c', 'cpar')], writes=[('R5', 'sig', 2)])
                    P.op('dve', I('tensor_tensor', out=MIX[:, c, c0:c0 + n], in0=a, in1=sga,
                                                                                       op=ALU.mult),
                         reads=[('R5', 'sig', 2), ('R5', 'sig', 0)], writes=mixkeys([c], ch))

        _phase_end('ga')
        P.fence(['R3', 'R5', 'mst'])
        QT = R3[:, 0:4096].rearrange("p (a t) -> p a t", a=2)
        KT = R3[:, 4096:8192].rearrange("p (a t) -> p a t", a=2)
        VV = R3[:, 8192:8192 + 16 * 256].rearrange("p (t a d) -> p t a d", t=16, a=2)
        QTS0 = qTs[:, 0:256].rearrange("p (a t) -> p a t", a=8)
        KTS0 = kTs[:, 0:256].rearrange("p (a t) -> p a t", a=8)
        VS0 = None
        ptb = [R5[:, 4608:5120], R5[:, 5120:5632]]
        e1 = R5f[:, 0:512]
        e2 = R5f[:, 512:1024]
        e3 = R5[:, 4096:4608]
        qf = R5f[:, 1024:1280]
        qr = R5[:, 5632:5888]
        kr = R5f[:, 1280:1536]
        ROPE_TMP = 1536
        mu = maskul[:, 0:128]
        rcnt = [0]
        for hp in range(4):
            sl_q = 12 + 4 * hp
            for kind in range(3):
                slab, bt, skey, bkey = S0.get(sl_q + kind)
                for t in range(16):
                    rows = tile_rows(t)
                    bank = t % 2
                    proj_tm(XTA, 'xT', slab, skey, (lambda kc, t=t, rows=rows: XTA[:, kc, t * 128:t * 128 + rows]), rows, bank, 0, 256, [t])
                    if kind == 2:
                        P.op('dve', I('tensor_tensor', out=qf[0:rows, :], in0=PS(bank)[0:rows, 0:256],
                                                                                      in1=bt[0:rows, :], op=ALU.add),
                             reads=[pk(bank), bkey], writes=[('R5', 'qf')])
                        dst = vb_p[hp, t * 128:t * 128 + rows, :] if t < 16 else vb_s[hp, :, :]
                        final_toks.append(P.dma('sp', I('dma_start', out=dst, in_=qf[0:rows, :]),
                                                ('R5', 'qfd'), reads=[('R5', 'qf')]))
                        if t < 16:
                            P.op('act', I('copy', out=VV[:, t, :, :], in_=qf[:, :].rearrange("p (a d) -> p a d", a=2)),
                                 reads=[('R5', 'qf')], writes=[('R3', 'V', t)])
                        else:
                            P.op('act', I('copy', out=VS0[:, 2 * hp:2 * hp + 2, :],
                                                                in_=qf[0:32, :].rearrange("p (a d) -> p a d", a=2)),
                                 reads=[('R5', 'qf')], writes=[('c', 'Vs', hp)])
                        continue
                    P.op('dve', I('tensor_tensor', out=qf[0:rows, :], in0=PS(bank)[0:rows, 0:256],
                                                                                  in1=bt[0:rows, :], op=ALU.add),
                         reads=[pk(bank), bkey], writes=[('R5', 'qf')])
                    rslot = rcnt[0] % 2
                    rcnt[0] += 1
                    rkey = ('c', 'ropeT', rslot)
                    P.dma('sp', I('dma_start', out=ropeT[rslot][0:rows, 0:64], in_=rope0[0:rows, 64 * t:64 * t + 64]),
                          rkey, writes=[rkey])
                    cosv = ropeT[rslot][0:rows, 0:32]
                    sinv = ropeT[rslot][0:rows, 32:64]
                    src3 = qf[0:rows, :].rearrange("p (a d) -> p a d", a=4)
                    if kind == 0:
                        dst3 = qr[0:rows, :].rearrange("p (a d) -> p a d", a=4)
                        wk = rope_ops(src3, dst3, rows, 4, 32, cosv, sinv, ('R5', 'qf'), ('R5', 'qr'), ROPE_TMP, rkey)
                        srcT = qr
                    else:
                        dst3 = kr[0:rows, :].rearrange("p (a d) -> p a d", a=4)
                        wk = rope_ops(src3, dst3, rows, 4, 32, cosv, sinv, ('R5', 'qf'), ('R5', 'kr'), ROPE_TMP, rkey)
                        dst = kb_p[hp, t * 128:t * 128 + rows, :] if t < 16 else kb_s[hp, :, :]
                        final_toks.append(P.dma('sp', I('dma_start', out=dst, in_=kr[0:rows, :]),
                                                ('R5', 'krd'), reads=wk))
                        P.op('act', I('copy', out=qr[0:rows, :], in_=kr[0:rows, :]), reads=wk,
                             writes=[('R5', 'qr', 'a'), ('R5', 'qr', 'b')])
                        srcT = qr
                    tb = 2 + t % 2
                    pv = PSB(tb)[:, 0:256].rearrange("p (a b) -> p a b", a=2)
                    fns = [I('transpose', out=pv[:, a, 0:rows], in_=srcT[0:rows, a * 128:(a + 1) * 128],
                                                                        identity=identb[0:rows, 0:rows]) for a in range(2)]
                    P.op('pe', fns, reads=[('R5', 'qr', 'a'), ('R5', 'qr', 'b'), ('c', 'identb')], writes=[pk(tb)])
                    if t < 16:
                        dstT = (QT if kind == 0 else KT)[:, :, t * 128:(t + 1) * 128]
                        wkey = [('R3', 'QT' if kind == 0 else 'KT', t)]
                    else:
                        dstT = (QTS0 if kind == 0 else KTS0)[:, 2 * hp:2 * hp + 2, :]
                        wkey = [('c', 'qTs' if kind == 0 else 'kTs', hp)]
                    P.op('act', I('copy', out=dstT, in_=pv[:, :, 0:rows]),
                         reads=[pk(tb)], writes=wkey)
            for hh in range(2):
                h = 2 * hp + hh
                scnt = 0
                for ch in range(4):
                    for m in range(2):
                        bO, bD = 4 + 2 * m, 5 + 2 * m
                        lastkb = 4 * ch + 3
                        for kb in range(lastkb + 1):
                            j = kb - 4 * ch
                            col0 = 128 * max(j, 0)
                            n = 512 - col0
                            sbk = 2 + scnt % 2
                            slot = scnt % 2
                            scnt += 1
                            qtiles = list(range(4 * ch + max(j, 0), 4 * ch + 4))
                            P.op('pe', I('matmul',
                                PS(sbk)[:, 0:n], lhsT=KT[64 * m:64 * m + 64, hh, kb * 128:(kb + 1) * 128],
                                rhs=QT[64 * m:64 * m + 64, hh, 512 * ch + col0:512 * ch + 512], start=True, stop=True),
                                reads=[('R3', 'KT', kb)] + [('R3', 'QT', tq) for tq in qtiles], writes=[pk(sbk)])
                            pt_ = ptb[slot]
                            P.op('act', I('activation', out=pt_[:, 0:n], in_=PS(sbk)[:, 0:n], func=AF.Exp,
                                                                                  scale=SC_B),
                                 reads=[pk(sbk)], writes=[('R5', 'pt', slot)])
                            if j >= 0:
                                P.op('dve', I('tensor_tensor', out=pt_[:, 0:128], in0=pt_[:, 0:128], in1=mu, op=ALU.mult),
                                     reads=[('R5', 'pt', slot), ('c', 'maskul')], writes=[('R5', 'pt', slot)])
                            P.op('pe', [I('matmul',
                                PS(bO)[:, col0:512], lhsT=VV[:, kb, hh, :], rhs=pt_[:, 0:n], start=(kb == 0), stop=(kb == lastkb)),
                                I('matmul',
                                    PS(bD)[:, col0:512], lhsT=onesb[:], rhs=pt_[:, 0:n], start=(kb == 0), stop=(kb == lastkb))],
                                reads=[('R5', 'pt', slot), ('R3', 'V', kb), ('c', 'ones')], writes=[pk(bO), pk(bD)])
                    P.op('dve', I('reciprocal', out=e1, in_=PS(5)[:, :]), reads=[pk(5)], writes=[('R5', 'e1')])
                    P.op('dve', I('tensor_tensor', out=e1, in0=PS(4)[:, :], in1=e1, op=ALU.mult), reads=[pk(4), ('R5', 'e1')],
                         writes=[('R5', 'e1')])
                    P.op('dve', I('reciprocal', out=e2, in_=PS(7)[:, :]), reads=[pk(7)], writes=[('R5', 'e2')])
                    P.op('dve', I('tensor_tensor', out=e2, in0=PS(6)[:, :], in1=e2, op=ALU.mult), reads=[pk(6), ('R5', 'e2')],
                         writes=[('R5', 'e2')])
                    P.op('dve', I('scalar_tensor_tensor', out=e1, in0=e2, scalar=lsm[:, 5:6], in1=e1, op0=ALU.mult, op1=ALU.add),
                         reads=[('R5', 'e1'), ('R5', 'e2'), ('c', 'neglam')], writes=[('R5', 'e1')])
                    P.op('act', I('activation', out=e3, in_=e1, func=AF.Square), reads=[('R5', 'e1')], writes=[('R5', 'e3')])
                    rb = ch % 2
                    P.op('pe', I('matmul', PS(rb)[:, :], lhsT=onesb[:], rhs=e3, start=True, stop=True),
                         reads=[('R5', 'e3'), ('c', 'ones')], writes=[pk(rb)])
                    P.op('act', I('activation', out=e2, in_=PS(rb)[:, :], func=AF.Sqrt, scale=1.0 / 128, bias=LN_EPS),
                         reads=[pk(rb)], writes=[('R5', 'e2')])
                    P.op('dve', I('reciprocal', out=e2, in_=e2), reads=[('R5', 'e2')], writes=[('R5', 'e2')])
                    P.op('dve', I('scalar_tensor_tensor', out=MIX[:, 8 + h, 512 * ch:512 * ch + 512], in0=e1,
                                                                             scalar=lsm[:, 6:7], in1=e2, op0=ALU.mult, op1=ALU.mult),
                         reads=[('R5', 'e1'), ('R5', 'e2'), ('c', 'subgs')], writes=mixkeys([8 + h], ch))
            slab, _, skey, _ = S0.get(sl_q + 3)
            for hh in range(2):
                h = 2 * hp + hh
                for ch in range(4):
                    c0, n = chunk_cols(ch)
                    bank = ch % 2
                    proj_fm(XTA, 'xT', slab, skey, hh, ch, bank)
                    bcol = bfm[:, 2 * (sl_q + 3) + hh:2 * (sl_q + 3) + hh + 1]
                    if ch < 4:
                        P.op('act', I('activation', out=e1, in_=PS(bank)[:, :], func=AF.Silu, bias=bcol),
                             reads=[pk(bank), ('c', 'bfm')], writes=[('R5', 'e1')])
                        P.op('dve', I('tensor_tensor', out=MIX[:, 8 + h, c0:c0 + 512], in0=MIX[:, 8 + h, c0:c0 + 512],
                                                                          in1=e1, op=ALU.mult),
                             reads=[('R5', 'e1')] + mixkeys([8 + h], ch), writes=mixkeys([8 + h], ch))
                    else:
                        P.op('act', I('activation', out=MIX[:, 8 + h, 2048:2080], in_=PS(bank)[:, 0:32],
                                                                                  func=AF.Silu, bias=bcol),
                             reads=[pk(bank), ('c', 'bfm')], writes=mixkeys([8 + h], 4))

        _phase_end('attnB')
        P.fence(['R3', 'R5', 'slab', 'btm'])
        XSA = R3[:, 0:4096].rearrange("p (k t) -> p k t", k=16)
        xsg = [R3[:, 4096:6144], R3[:, 6144:8192]]
        GB = R3f[:, 2048:2304]
        RES = R3f[:, 2304:2560]
        QA = R3[:, 5120:5632].rearrange("p (s j) -> p s j", s=32)
        RT = R3f[:, 2816:3072].rearrange("p (u d) -> p u d", u=2)
        GG = R3f[:, 4096:6144].rearrange("p (a d) -> p a d", a=16)
        KP = [R5[:, 128 * i:128 * (i + 1)] for i in range(8)]
        VP = [R5[:, 1024 + 128 * i:1024 + 128 * (i + 1)] for i in range(8)]
        KT4 = [R5[:, 2048:2560], R5[:, 2560:3072]]
        PM = R5[0:16, 3072:3584]
        PT = R5[:, 3584:3648]
        PTn = R5[0:8, 3648:3664]
        PMn = R5[0:16, 3664:3672]
        qkvf = R5f[0:8, 1856:2240]
        krs = R5f[0:8, 2240:2368]
        SB_ROPE_TMP = 2368
        rs = R5f[0:16, 2624:2644]
        o32 = R5f[0:16, 2648:2776]
        f1 = R5f[:, 2776:2784]
        f2 = R5f[:, 2784:2792]
        X2 = slabs[2][:].rearrange("p a b -> p (a b)")
        X2f = X2.bitcast(F32)
        X2i = X2.bitcast(I32)
        btmS = X2f[0:8, 0:384]
        ropS = X2f[0:8, 384:448]
        pts = X2i[:, 448:512]
        pidx_s = X2i[:, 512:576]
        SEL = X2f[:, 576:640].rearrange("p (u t) -> p u t", u=2)
        cm2 = X2f[0:16, 648:656]
        c12s = X2f[0:16, 656:672]
        smask8 = X2[0:16, 1280:1288]
        qrs = X2[0:8, 1344:1472]
        krb = X2[0:8, 1472:1600]
        VN = [X2[0:8, 1600:1728], X2[0:8, 1728:1856]]
        KN = [X2[:, 1856:1864], X2[:, 1864:1872]]
        f3 = X2[:, 1872:1880]
        SK = ('slab', 2)
        P.dma('sp', I('dma_start', out=btmS, in_=bsb_tm.partition_broadcast(8)), ('x2', 'a'), writes=[('x2', 'btmS')])
        P.dma('sp', I('dma_start', out=ropS, in_=rope_s), ('x2', 'a'), writes=[('x2', 'ropS')])
        P.dma('sp', I('dma_start', out=SEL, in_=sel_in.rearrange("p (u t) -> p u t", u=2)), ('x2', 'a'), writes=[('x2', 'SEL')])
        P.dma('sp', I('dma_start', out=c12s, in_=c12s_in), ('x2', 'a'), writes=[('x2', 'c12s')])
        P.dma('pool', I('dma_start', out=smask8, in_=smask8_in), ('x2', 'b'), writes=[('x2', 'smask8')])
        P.op('dve', I('scalar_tensor_tensor', out=cm2, in0=c12s[:, 8:16], scalar=lsm[0:16, 5:6], in1=c12s[:, 0:8],
                      op0=ALU.mult, op1=ALU.add), reads=[('x2', 'c12s'), ('c', 'neglam')], writes=[('x2', 'cm2')])
        for tt in range(2):
            P.dma('pool', I('dma_start', out=xsg[tt], in_=xsa[128 * tt:128 * tt + 128, :]), ('R3', 'xsg', tt), writes=[('R3', 'xsg', tt)])
            for hf in range(2):
                pv = PSB(hf).rearrange("p (a b) -> p a b", a=8)
                fns = [I('transpose', out=pv[:, kc % 8, :], in_=xsg[tt][:, kc * 128:(kc + 1) * 128], identity=identb[:])
                       for kc in range(8 * hf, 8 * hf + 8)]
                P.op('pe', fns, reads=[('R3', 'xsg', tt), ('c', 'identb')], writes=[pk(hf)])
                if hf == 0:
                    P.op('act', I('copy', out=XSA[:, 0:8, 128 * tt:128 * tt + 128], in_=pv), reads=[pk(hf)], writes=[('R3', 'XSA', tt, 0)])
                else:
                    P.op('dve', I('tensor_copy', out=XSA[:, 8:16, 128 * tt:128 * tt + 128], in_=pv), reads=[pk(hf)],
                         writes=[('R3', 'XSA', tt, 1)])
        xsk = [('R3', 'XSA', tt, hf) for tt in range(2) for hf in range(2)]
        P.op('dve', I('memset', QA, 0.0), writes=[('R3', 'QA'), ('R3', 'xsg', 0), ('R3', 'xsg', 1)])
        P.dma('pool', I('dma_start', out=slabs[0][:], in_=wsb[:, :, 0:256]), ('slab', 0), writes=[('slab', 0)])
        P.dma('pool', I('dma_start', out=slabs[1][:], in_=wsb[:, :, 256:512]), ('slab', 1), writes=[('slab', 1)])
        fns = [I('matmul', PS(0)[:, 0:256], lhsT=slabs[1][:, kc, 128:256], rhs=XSA[:, kc, :], start=(kc == 0), stop=(kc == 15))
               for kc in range(16)]
        P.op('pe', fns, reads=xsk + [('slab', 1)], writes=[pk(0)])
        P.op('act', I('activation', out=GB, in_=PS(0)[:, 0:256], func=AF.Silu, bias=bsb_fm_sb[:, 0:1]), reads=[pk(0), ('c', 'bsbfm')],
             writes=[('R3', 'GB')])
        pcnt = 0
        for s in range(32):
            par = s % 2
            fns = [I('matmul', PS(0)[0:8, 0:256], lhsT=XSA[:, kc, 8 * s:8 * s + 8], rhs=slabs[0][:, kc, :], start=(kc == 0), stop=(kc == 15))
                   for kc in range(16)]
            fns += [I('matmul', PS(0)[0:8, 256:384], lhsT=XSA[:, kc, 8 * s:8 * s + 8], rhs=slabs[1][:, kc, 0:128], start=(kc == 0),
                      stop=(kc == 15)) for kc in range(16)]
            P.op('pe', fns, reads=xsk + [('slab', 0), ('slab', 1)], writes=[pk(0)])
            P.op('dve', I('tensor_tensor', out=qkvf, in0=PS(0)[0:8, 0:384], in1=btmS, op=ALU.add), reads=[pk(0), ('x2', 'btmS')],
                 writes=[('R5', 'qkvf')])
            cosv = ropS[:, 0:32]
            sinv = ropS[:, 32:64]
            wq = rope_ops(qkvf[:, 0:128].rearrange("p (a d) -> p a d", a=2), qrs.rearrange("p (a d) -> p a d", a=2), 8, 2, 32,
                          cosv, sinv, ('R5', 'qkvf'), ('x2', 'qrs'), SB_ROPE_TMP, ('x2', 'ropS'))
            wk = rope_ops(qkvf[:, 128:256].rearrange("p (a d) -> p a d", a=2), krs.rearrange("p (a d) -> p a d", a=2), 8, 2, 32,
                          cosv, sinv, ('R5', 'qkvf'), ('R5', 'krs'), SB_ROPE_TMP, ('x2', 'ropS'))
            final_toks.append(P.dma('sp', I('dma_start', out=kb_sh[8 * s:8 * s + 8, :], in_=krs), ('R5', 'krsd'), reads=wk))
            final_toks.append(P.dma('sp', I('dma_start', out=vb_sh[8 * s:8 * s + 8, :], in_=qkvf[:, 256:384]), ('R5', 'qkvd'),
                                    reads=[('R5', 'qkvf')]))
            P.op('act', I('copy', out=krb, in_=krs), reads=wk, writes=[('x2', 'krb')])
            P.op('act', I('copy', out=VN[par], in_=qkvf[:, 256:384]), reads=[('R5', 'qkvf')], writes=[('x2', 'VN', par)])
            pv5 = PSB(5)[:, 0:16]
            P.op('pe', [I('transpose', out=pv5[:, 0:8], in_=qrs, identity=identb[0:8, 0:8]),
                        I('transpose', out=pv5[:, 8:16], in_=krb, identity=identb[0:8, 0:8])],
                 reads=wq + [('x2', 'krb'), ('c', 'identb')], writes=[pk(5)])
            P.op('dve', I('tensor_copy', out=QA[0:64, s, 0:8], in_=pv5[0:64, 0:8]), reads=[pk(5)], writes=[('R3', 'QA', s, 0)],
                 extra=[P.lastw.get(('R3', 'QA'))])
            P.op('dve', I('tensor_copy', out=QA[64:128, s, 8:16], in_=pv5[64:128, 0:8]), reads=[pk(5)], writes=[('R3', 'QA', s, 1)])
            P.op('act', I('copy', out=KN[par], in_=pv5[:, 8:16]), reads=[pk(5)], writes=[('x2', 'KN', par)])
            qak = [('R3', 'QA', s, 0), ('R3', 'QA', s, 1)]
            P.dma('sp', I('dma_start', out=pts, in_=pt_in[0:1, 64 * s:64 * s + 64].partition_broadcast(128)), ('x2', 'pts'),
                  writes=[('x2', 'pts')])
            P.op('pool', I('tensor_scalar', out=pidx_s, in0=pts, scalar1=128, scalar2=iot[:, 0:1], op0=ALU.mult, op1=ALU.add),
                 reads=[('x2', 'pts'), ('c', 'iot')], writes=[('x2', 'pidx')])
            for grp in range(16):
                gb_ = grp % 2
                tb = 1 if gb_ == 0 else 6
                sbk = 2 if gb_ == 0 else 7
                slots = []
                for pg in range(4):
                    jpage = 4 * grp + pg
                    slot = pcnt % 8
                    pcnt += 1
                    slots.append(slot)
                    P.dma('pool', I('indirect_dma_start', out=KP[slot], out_offset=None, in_=kbh,
                                    in_offset=bass.IndirectOffsetOnAxis(ap=pidx_s[:, jpage:jpage + 1], axis=0)),
                          ('R5', 'kp', slot), reads=[('x2', 'pidx')], writes=[('R5', 'kp', slot)])
                    P.dma('pool', I('indirect_dma_start', out=VP[slot], out_offset=None, in_=vbh,
                                    in_offset=bass.IndirectOffsetOnAxis(ap=pidx_s[:, jpage:jpage + 1], axis=0)),
                          ('R5', 'vp', slot), reads=[('x2', 'pidx')], writes=[('R5', 'vp', slot)])
                fns = [I('transpose', out=PSB(tb)[:, 128 * pg:128 * pg + 128], in_=KP[slots[pg]], identity=identb[:]) for pg in range(4)]
                P.op('pe', fns, reads=[('R5', 'kp', sl) for sl in slots] + [('c', 'identb')], writes=[pk(tb)])
                if gb_ == 0:
                    P.op('act', I('copy', out=KT4[gb_], in_=PSB(tb)[:, 0:512]), reads=[pk(tb)], writes=[('R5', 'kt4', gb_)])
                else:
                    P.op('dve', I('tensor_copy', out=KT4[gb_], in_=PSB(tb)[:, 0:512]), reads=[pk(tb)], writes=[('R5', 'kt4', gb_)])
                P.op('pe', I('matmul', PS(sbk)[0:16, 0:512], lhsT=QA[:, s, :], rhs=KT4[gb_], start=True, stop=True),
                     reads=qak + [('R5', 'kt4', gb_)], writes=[pk(sbk)])
                P.op('act', I('activation', out=PM, in_=PS(sbk)[0:16, 0:512], func=AF.Exp, scale=SC_B, accum_out=rs[:, grp:grp + 1]),
                     reads=[pk(sbk)], writes=[('R5', 'pm'), ('R5', 'rs', grp)])
                fns = [I('transpose', out=PSB(3)[:, 16 * pg:16 * pg + 16], in_=PM[:, 128 * pg:128 * pg + 128], identity=identb[0:16, 0:16])
                       for pg in range(4)]
                P.op('pe', fns, reads=[('R5', 'pm'), ('c', 'identb')], writes=[pk(3)])
                P.op('dve', I('tensor_copy', out=PT, in_=PSB(3)[:, 0:64]), reads=[pk(3)], writes=[('R5', 'ptb')])
                fns = [I('matmul', PS(4)[0:16, 0:128], lhsT=PT[:, 16 * pg:16 * pg + 16], rhs=VP[slots[pg]],
                         start=(grp == 0 and pg == 0), stop=False) for pg in range(4)]
                P.op('pe', fns, reads=[('R5', 'ptb')] + [('R5', 'vp', sl) for sl in slots], writes=[pk(4)])
            P.op('pe', I('matmul', PS(2)[0:16, 0:8], lhsT=QA[:, s, :], rhs=KN[par], start=True, stop=True),
                 reads=qak + [('x2', 'KN', par)], writes=[pk(2)])
            P.op('act', I('activation', out=PMn, in_=PS(2)[0:16, 0:8], func=AF.Exp, scale=SC_B), reads=[pk(2)], writes=[('R5', 'pmn')])
            P.op('dve', I('tensor_tensor', out=PMn, in0=PMn, in1=smask8, op=ALU.mult), reads=[('R5', 'pmn'), ('x2', 'smask8')],
                 writes=[('R5', 'pmn')])
            P.op('dve', I('reduce_sum', out=rs[:, 16:17], in_=PMn, axis=AX.X), reads=[('R5', 'pmn')], writes=[('R5', 'rs', 16)])
            P.op('pe', I('transpose', out=PSB(3)[0:8, 0:16], in_=PMn, identity=identb[0:16, 0:16]), reads=[('R5', 'pmn'), ('c', 'identb')],
                 writes=[pk(3)])
            P.op('dve', I('tensor_copy', out=PTn, in_=PSB(3)[0:8, 0:16]), reads=[pk(3)], writes=[('R5', 'ptn')])
            P.op('pe', I('matmul', PS(4)[0:16, 0:128], lhsT=PTn, rhs=VN[par], start=False, stop=True),
                 reads=[('R5', 'ptn'), ('x2', 'VN', par)], writes=[pk(4)])
            P.op('dve', I('reduce_sum', out=rs[:, 17:18], in_=rs[:, 0:17], axis=AX.X), reads=[('R5', 'rs', g) for g in range(17)],
                 writes=[('R5', 'rs', 17)])
            P.op('dve', I('reciprocal', out=rs[:, 18:19], in_=rs[:, 17:18]), reads=[('R5', 'rs', 17)], writes=[('R5', 'rs', 18)])
            P.op('dve', I('tensor_single_scalar', out=o32, in_=PS(4)[0:16, 0:128], scalar=rs[:, 18:19], op=ALU.mult),
                 reads=[pk(4), ('R5', 'rs', 18)], writes=[('R5', 'o32')])
            P.op('pe', I('matmul', PS(5)[:, 32:40], lhsT=o32, rhs=cm2, start=True, stop=True), reads=[('R5', 'o32'), ('x2', 'cm2')],
                 writes=[pk(5)])
            P.op('act', I('copy', out=f1, in_=PS(5)[:, 32:40]), reads=[pk(5)], writes=[('R5', 'f1')])
            P.op('act', I('activation', out=f3, in_=f1, func=AF.Square), reads=[('R5', 'f1')], writes=[('x2', 'f3')])
            P.op('pe', I('matmul', PS(5)[:, 48:56], lhsT=onesb[:], rhs=f3, start=True, stop=True), reads=[('x2', 'f3'), ('c', 'ones')],
                 writes=[pk(5)])
            P.op('act', I('activation', out=f2, in_=PS(5)[:, 48:56], func=AF.Sqrt, scale=1.0 / 128, bias=LN_EPS), reads=[pk(5)],
                 writes=[('R5', 'f2')])
            P.op('dve', I('reciprocal', out=f2, in_=f2), reads=[('R5', 'f2')], writes=[('R5', 'f2')])
            P.op('dve', I('scalar_tensor_tensor', out=f1, in0=f1, scalar=lsm[:, 6:7], in1=f2, op0=ALU.mult, op1=ALU.mult),
                 reads=[('R5', 'f1'), ('R5', 'f2'), ('c', 'subgs')], writes=[('R5', 'f1')])
            P.op('dve', I('tensor_tensor', out=RES[:, 8 * s:8 * s + 8], in0=f1, in1=GB[:, 8 * s:8 * s + 8], op=ALU.mult),
                 reads=[('R5', 'f1'), ('R3', 'GB')], writes=[('R3', 'RES', s)])
        for u in range(2):
            P.op('pe', I('transpose', out=PS(u)[:, 0:128], in_=RES[:, 128 * u:128 * u + 128], identity=identf[:]),
                 reads=[('R3', 'RES', s) for s in range(16 * u, 16 * u + 16)] + [('c', 'identf')], writes=[pk(u)])
            P.op('act', I('copy', out=RT[:, u, :], in_=PS(u)[:, 0:128]), reads=[pk(u)], writes=[('R3', 'RT', u)])
        tsrc = P.dma('pool', I('dma_start', out=ccsrc.rearrange("(u p) d -> p u d", p=128), in_=RT), ('cc', 'src'),
                     reads=[('R3', 'RT', 0), ('R3', 'RT', 1)], writes=[('cc', 'srcd')])
        if ncores > 1:
            P.wait('pool', tsrc)
            ccs = P.stack.enter_context(nc.semaphore("ccsem"))
            if not P.dead:
                P.q['pool'].append(lambda E: E.collective_compute("AllGather", ALU.bypass, replica_groups=[list(range(ncores))],
                                                              ins=[ccsrc_t.ap().opt()], outs=[ccdst_t.ap().opt()]).then_inc(ccs))
            tcc = (ccs, 1, 'cc')
            P.lastw[('cc', 'dst')] = tcc
        else:
            P.dma('pool', I('dma_start', out=ccdst[0:256, :], in_=ccsrc), ('cc', 'cp'), reads=[('cc', 'srcd')], writes=[('cc', 'dst')])
        P.dma('sp', I('dma_start', out=GG, in_=ccdst.rearrange("(a p) d -> p a d", p=128)), ('R3', 'GG'), reads=[('cc', 'dst')],
              writes=[('R3', 'GG')])
        for hd in range(8):
            bank = hd % 2
            fns = [I('matmul', PS(bank)[:, 0:32], lhsT=GG[:, 2 * hd + u, :], rhs=SEL[:, u, :], start=(u == 0), stop=(u == 1))
                   for u in range(2)]
            P.op('pe', fns, reads=[('R3', 'GG'), ('x2', 'SEL')], writes=[pk(bank)])
            P.op('act', I('copy', out=MIX[:, 8 + hd, 2048:2080], in_=PS(bank)[:, 0:32]), reads=[pk(bank)], writes=mixkeys([8 + hd], 4))

        _phase_end('sampleB')
        P.fence(['R3', 'R5', 'xT'])
        for cc in range(16):
            P.dma('pool', I('dma_start', out=WO0[:, cc, :], in_=wo0[:, cc, :]), ('wo', cc % 4), writes=[('wo', 'w')])
        S1 = None
        outproj_layer(0, MIX, 16, WO0, ('wo', 'w'),
                      (lambda t, rows, nb: (xp[t * 128:t * 128 + rows, nb * 512:(nb + 1) * 512] if t < 16
                                            else xs[:, nb * 512:(nb + 1) * 512])), True)
        P.fence(['R3', 'R5', 'wo', 'mix', 'x1T', 'x1d', 'ps', 'c', 'slab', 'btm', 'x2', 'cc'])

        _phase_end('out0')
        if skip_l0:
            P.dead = False
            tokm = P.op('dve', I('memset', bufB[:], 0.25), writes=[('x1T', 'all')])
            for e_ in P.ENG:
                P.wait(e_, tokm)
        X1 = MIX
        ld('sp', bfm[:, 0:80], bfm1, 'bfm')
        TM1 = set()
        for h in range(8):
            TM1.update([5 * h + i for i in range(5)])
        S1 = Slabs(w1, btm1, 40, 'w1', TM1)
        S1.pref(2)
        _phase_end('l1start')
        QT1 = [R7[:, 2048 * (2 * g):2048 * (2 * g + 1)] for g in range(3)]
        KT1 = [R7[:, 2048 * (2 * g + 1):2048 * (2 * g + 2)] for g in range(3)]
        V1 = [R3[:, 2048 * g:2048 * (g + 1)].rearrange("p (t d) -> p t d", t=16) for g in range(3)]
        GT = R3[:, 6144:6144 + NTOK]
        QTS1 = qTs[:].rearrange("p (g h t) -> p g h t", g=3, h=8)
        KTS1 = kTs[:].rearrange("p (g h t) -> p g h t", g=3, h=8)
        VS1 = R7[0:32, 12288:12288 + 3072].rearrange("p (g h d) -> p g h d", g=3, h=8)
        GTS = gTs[:].rearrange("p (h t) -> p h t", h=8)
        ptc = [R5[:, 4608:5120], R5[:, 5120:5632]]
        ML = maskul[:, 128:256]

        def x1k(t):
            return [('x1T', t, 0), ('x1T', t, 1)]

        def proj_tm1(slab, skey, tok_ap_fn, rows, bank, c0, ncols, xtiles):
            fns = [I('matmul', PS(bank)[0:rows, 0:ncols], lhsT=tok_ap_fn(kc), rhs=slab[:, kc, c0:c0 + ncols],
                                             start=(kc == 0), stop=(kc == 15)) for kc in range(16)]
            rk = [skey]
            for t in xtiles:
                rk += x1k(t)
            P.op('pe', fns, reads=rk, writes=[pk(bank)])

        def vtile_tokens(g, u):
            if g == 0:
                return (128 * u, 128 * u + 128, 1), [u], ((0, 128, 1) if u == 15 else None)
            if g == 1:
                sbk, r = u // 4, u % 4
                return (512 * sbk + r, 512 * sbk + 512, 4), list(range(4 * sbk, 4 * sbk + 4)), ((r, 512, 4) if sbk == 3 else None)
            return (u, 2048, 16), list(range(16)), (u, 2048, 16)

        for h in range(8):
            for g in range(3):
                slab, bt, skey, bkey = S1.get(5 * h + g)
                W = C_W[g]
                for t in range(NT):
                    rows = tile_rows(t)
                    bank = t % 2
                    proj_tm1(slab, skey, (lambda kc, t=t, rows=rows: X1[:, kc, t * 128:t * 128 + rows]), rows, bank, 0, 256, [t])
                    P.op('dve', I('tensor_tensor', out=qf[0:rows, :], in0=PS(bank)[0:rows, 0:256],
                                                                                  in1=bt[0:rows, :], op=ALU.add),
                         reads=[pk(bank), bkey], writes=[('R5', 'qf')])
                    rslot = rcnt[0] % 2
                    rcnt[0] += 1
                    rkey = ('c', 'ropeT', rslot)
                    P.dma('sp', I('dma_start', out=ropeT[rslot][0:rows, :], in_=rope1[0:rows, 128 * t:128 * t + 128]),
                          rkey, writes=[rkey])
                    cosv = ropeT[rslot][0:rows, 0:64]
                    sinv = ropeT[rslot][0:rows, 64:128]
                    src3 = qf[0:rows, :].rearrange("p (a d) -> p a d", a=2)
                    _phase_end('l1proj')
                    dq = qr[0:rows, 0:128].rearrange("p (a d) -> p a d", a=1)
                    wq = rope_ops(src3[:, 0:1, :], dq, rows, 1, 64, cosv, sinv, ('R5', 'qf'), ('R5', 'qr'), ROPE_TMP, rkey)
                    dk = kr[0:rows, 0:128].rearrange("p (a d) -> p a d", a=1)
                    wk = rope_ops(src3[:, 1:2, :], dk, rows, 1, 64, cosv, sinv, ('R5', 'qf'), ('R5', 'kr'), ROPE_TMP + 256, rkey)
                    _phase_end('l1rope')
                    if t < 16:
                        pos0 = 128 * t - (T - min(W, T))
                        if pos0 >= 0:
                            final_toks.append(P.dma('sp', I('dma_start',
                                out=kvc_p[g][0, h, pos0:pos0 + 128, :], in_=kr[:, 0:128]), ('R5', 'krd'), reads=wk))
                    else:
                        for s in range(4):
                            final_toks.append(P.dma('sp', I('dma_start',
                                out=kvc_s[g][s, W - 8:W, 0, h * 128:(h + 1) * 128], in_=kr[8 * s:8 * s + 8, 0:128]),
                                ('R5', 'krd'), reads=wk))
                    P.op('act', I('copy', out=qr[0:rows, 128:256], in_=kr[0:rows, 0:128]), reads=wk,
                         writes=[('R5', 'qr', 'k')])
                    pv = PSB(2)[:, 0:256].rearrange("p (a b) -> p a b", a=2)
                    fns = [I('transpose', out=pv[:, a, 0:rows], in_=qr[0:rows, a * 128:(a + 1) * 128],
                                                                        identity=identb[0:rows, 0:rows]) for a in range(2)]
                    _phase_end('l1cp')
                    P.op('pe', fns, reads=wq + [('R5', 'qr', 'k'), ('c', 'identb')], writes=[pk(2)])
                    _phase_end('l1tr')
                    if t < 16:
                        P.op('act', I('copy', out=QT1[g][:, t * 128:(t + 1) * 128], in_=pv[:, 0, :]),
                             reads=[pk(2)], writes=[('R7', 'q', g, t)])
                        _phase_end('l1eq')
                        P.op('act', I('copy', out=KT1[g][:, t * 128:(t + 1) * 128], in_=pv[:, 1, :]),
                             reads=[pk(2)], writes=[('R7', 'k', g, t)])
                    else:
                        P.op('act', I('copy', out=QTS1[:, g, h, :], in_=pv[:, 0, 0:32]),
                             reads=[pk(2)], writes=[('c', 'qTs1', g, h)])
                        P.op('act', I('copy', out=KTS1[:, g, h, :], in_=pv[:, 1, 0:32]),
                             reads=[pk(2)], writes=[('c', 'kTs1', g, h)])
                    if t == 0:
                        _phase_end('l1t0')
                    if t == 15:
                        _phase_end('l1t15')
                _phase_end('l1g0')
            _phase_end('l1qk')
            for g in range(3):
                sidx = 5 * h + 3 + (0 if g < 2 else 1)
                slab, bt, skey, bkey = S1.get(sidx)
                c0 = 128 * (g % 2) if g < 2 else 0
                W = C_W[g]
                for u in range(17):
                    bank = u % 2
                    if u < 16:
                        (a0, a1, a2), xtiles, orow = vtile_tokens(g, u)
                        rows = 128
                        tokfn = (lambda kc, a0=a0, a1=a1, a2=a2: X1[:, kc, a0:a1:a2])
                    else:
                        rows = 32
                        xtiles = [16]
                        orow = None
                        tokfn = (lambda kc: X1[:, kc, 2048:2080])
                    proj_tm1(slab, skey, tokfn, rows, bank, c0, 128, xtiles)
                    P.op('dve', I('tensor_tensor',
                        out=qf[0:rows, 0:128], in0=PS(bank)[0:rows, 0:128], in1=bt[0:rows, c0:c0 + 128], op=ALU.add),
                        reads=[pk(bank), bkey], writes=[('R5', 'qf')])
                    if u < 16:
                        if orow is not None:
                            final_toks.append(P.dma('sp', I('dma_start',
                                out=kvc_p[g][1, h, orow[0]:orow[1]:orow[2], :], in_=qf[:, 0:128]), ('R5', 'qfd'), reads=[('R5', 'qf')]))
                        P.op('act', I('copy', out=V1[g][:, u, :], in_=qf[:, 0:128]), reads=[('R5', 'qf')],
                             writes=[('R3', 'V1', g, u)])
                    else:
                        for s in range(4):
                            final_toks.append(P.dma('sp', I('dma_start',
                                out=kvc_s[g][s, W - 8:W, 1, h * 128:(h + 1) * 128], in_=qf[8 * s:8 * s + 8, 0:128]),
                                ('R5', 'qfd'), reads=[('R5', 'qf')]))
                        P.op('act', I('copy', out=VS1[:, g, h, :], in_=qf[0:32, 0:128]), reads=[('R5', 'qf')],
                             writes=[('c', 'Vs1', g, h)])
            _phase_end('l1v')
            slab, bt, skey, bkey = S1.get(5 * h + 4)
            for ch in range(5):
                c0, n = chunk_cols(ch)
                bank = ch % 2
                fns = [I('matmul', PS(bank)[:, 0:n], lhsT=slab[:, kc, 128:256], rhs=X1[:, kc, c0:c0 + n],
                                                 start=(kc == 0), stop=(kc == 15)) for kc in range(16)]
                rk = [skey]
                for t in chunk_tiles(ch):
                    rk += x1k(t)
                P.op('pe', fns, reads=rk, writes=[pk(bank)])
                bcol = bfm[:, 2 * (5 * h + 4) + 1:2 * (5 * h + 4) + 2]
                if ch < 4:
                    P.op('act', I('activation', out=GT[:, c0:c0 + 512], in_=PS(bank)[:, :], func=AF.Silu,
                                                                                bias=bcol),
                         reads=[pk(bank), ('c', 'bfm')], writes=[('R3', 'GT', ch)])
                else:
                    P.op('act', I('activation', out=GTS[:, h, :], in_=PS(bank)[:, 0:32], func=AF.Silu,
                                                                              bias=bcol),
                         reads=[pk(bank), ('c', 'bfm')], writes=[('c', 'gTs', h)])
            _phase_end('l1gate')
            scnt = 0
            for ch in range(4):
                bN, bD = 5, 6
                steps = []
                for kb in range(max(4 * ch - 1, 0), 4 * ch + 4):
                    qb0 = max(kb, 4 * ch)
                    qb1 = min(kb + 1, 4 * ch + 3)
                    qcols = (128 * qb0 - 512 * ch, 128 * (qb1 + 1) - 512 * ch, 1)
                    masks = []
                    off = 0
                    for qb in range(qb0, qb1 + 1):
                        masks.append((off, mu if qb == kb else ML))
                        off += 128
                    steps.append(dict(g=0, kcols=(128 * kb, 128 * kb + 128, 1), qcols=qcols, n=off, masks=masks, vt=kb,
                                      kt=[kb], qt=list(range(qb0, qb1 + 1))))
                for r in range(4):
                    for sbk in (ch - 1, ch):
                        if sbk < 0:
                            continue
                        steps.append(dict(g=1, kcols=(512 * sbk + r, 512 * sbk + 512, 4), qcols=(r, 512, 4), n=128,
                                          masks=[(0, ML if sbk == ch - 1 else mu)], vt=4 * sbk + r,
                                          kt=list(range(4 * sbk, 4 * sbk + 4)), qt=list(range(4 * ch, 4 * ch + 4))))
                for r in range(16):
                    steps.append(dict(g=2, kcols=(r, 2048, 16), qcols=(r, 512, 16), n=32,
                                      masks=[(0, maskul[:, 32 * ch:32 * ch + 32])], vt=r,
                                      kt=list(range(16)), qt=list(range(4 * ch, 4 * ch + 4))))
                for si, sp_ in enumerate(steps):
                    g = sp_['g']; n = sp_['n']
                    k0, k1, k2 = sp_['kcols']
                    q0, q1, q2 = sp_['qcols']
                    sbk = 3 + scnt % 2
                    slot = scnt % 2
                    scnt += 1
                    P.op('pe', I('matmul',
                        PS(sbk)[:, 0:n], lhsT=KT1[g][:, k0:k1:k2], rhs=QT1[g][:, 512 * ch + q0:512 * ch + q1:q2], start=True, stop=True),
                        reads=[('R7', 'k', g, t) for t in sp_['kt']] + [('R7', 'q', g, t) for t in sp_['qt']], writes=[pk(sbk)])
                    pt_ = ptc[slot]
                    P.op('act', I('activation', out=pt_[:, 0:n], in_=PS(sbk)[:, 0:n], func=AF.Exp, scale=SC_C),
                         reads=[pk(sbk)], writes=[('R5', 'pt', slot)])
                    for (off, mk_) in sp_['masks']:
                        w = min(128, n)
                        P.op('dve', I('tensor_tensor', out=pt_[:, off:off + w], in0=pt_[:, off:off + w],
                                                                                          in1=mk_, op=ALU.mult),
                             reads=[('R5', 'pt', slot), ('c', 'maskul')], writes=[('R5', 'pt', slot)])
                    first = (si == 0)
                    last = (si == len(steps) - 1)
                    P.op('pe', [I('matmul',
                        PS(bN)[:, q0:q1:q2], lhsT=V1[g][:, sp_['vt'], :], rhs=pt_[:, 0:n], start=first, stop=last),
                        I('matmul',
                            PS(bD)[:, q0:q1:q2], lhsT=onesb[:], rhs=pt_[:, 0:n], start=first, stop=last)],
                        reads=[('R5', 'pt', slot), ('R3', 'V1', g, sp_['vt']), ('c', 'ones')], writes=[pk(bN), pk(bD)])
                P.op('dve', I('reciprocal', out=e1, in_=PS(6)[:, :]), reads=[pk(6)], writes=[('R5', 'e1')])
                P.op('dve', I('tensor_tensor', out=e1, in0=PS(5)[:, :], in1=e1, op=ALU.mult), reads=[pk(5), ('R5', 'e1')],
                     writes=[('R5', 'e1')])
                P.op('dve', I('tensor_tensor', out=MIX1[:, h, 512 * ch:512 * ch + 512], in0=e1,
                                                                  in1=GT[:, 512 * ch:512 * ch + 512], op=ALU.mult),
                     reads=[('R5', 'e1'), ('R3', 'GT', ch)], writes=[('mix1', t) for t in chunk_tiles(ch)])

        _phase_end('l1heads')
        P.fence(['R3', 'R5', 'R7'])
        KST = [R3[:, 1024 * i:1024 * (i + 1)] for i in range(2)]
        VST = [R3[:, 2048 + 1024 * i:2048 + 1024 * (i + 1)] for i in range(2)]
        KT2 = R3[:, 4096:5120].rearrange("p (a k) -> p a k", a=8)
        PE_ = [R5[:, 0:64], R5[:, 64:128]]
        o32s = R5f[:, 256:1280]
        rD = R5f[:, 1280:1288]
        tiles_c = [(0, 0, 128, 1)]
        tiles_c += [(1, r, 512, 4) for r in range(4)]
        tiles_c += [(2, r, 2048, 16) for r in range(8)]
        cnt = 0
        for s in range(4):
            first = True
            for ti, (g, r, Lb, dd) in enumerate(tiles_c):
                slot = cnt % 2
                cnt += 1
                P.dma('pool', I('dma_start', out=KST[slot], in_=skv[g][s, r:Lb:dd, 0, :]),
                      ('R3', 'kst', slot), writes=[('R3', 'kst', slot)])
                P.dma('pool', I('dma_start', out=VST[slot], in_=skv[g][s, r:Lb:dd, 1, :]),
                      ('R3', 'vst', slot), writes=[('R3', 'vst', slot)])
                tb = slot
                pv = PSB(tb).rearrange("p (a k) -> p a k", a=8)
                fns = [I('transpose', out=pv[:, a, :], in_=KST[slot][:, a * 128:(a + 1) * 128],
                                                                    identity=identb[:]) for a in range(8)]
                P.op('pe', fns, reads=[('R3', 'kst', slot), ('c', 'identb')], writes=[pk(tb)])
                P.op('act', I('copy', out=KT2, in_=pv), reads=[pk(tb)], writes=[('R3', 'kt2')])
                sbk = 2 + slot
                fns = [I('matmul', PS(sbk)[:, 8 * hd:8 * hd + 8], lhsT=KT2[:, hd, :],
                                                                    rhs=QTS1[:, g, hd, 8 * s:8 * s + 8], start=True, stop=True)
                       for hd in range(8)]
                P.op('pe', fns, reads=[('R3', 'kt2')] + [('c', 'qTs1', g, hd) for hd in range(8)], writes=[pk(sbk)])
                pe_ = PE_[slot]
                P.op('act', I('activation', out=pe_, in_=PS(sbk)[:, 0:64], func=AF.Exp, scale=SC_C),
                     reads=[pk(sbk)], writes=[('R5', 'pe', slot)])
                P.op('dve', I('tensor_tensor',
                    out=pe_.rearrange("p (a t) -> p a t", a=8), in0=pe_.rearrange("p (a t) -> p a t", a=8),
                    in1=cmask[:, 8 * ti:8 * ti + 8].unsqueeze(1).to_broadcast([128, 8, 8]), op=ALU.mult),
                    reads=[('R5', 'pe', slot), ('c', 'cmask')], writes=[('R5', 'pe', slot)])
                fns = []
                for hd in range(8):
                    fns.append(I('matmul',
                        PS(4 + hd // 4)[0:8, 128 * (hd % 4):128 * (hd % 4) + 128], lhsT=pe_[:, 8 * hd:8 * hd + 8],
                        rhs=VST[slot][:, 128 * hd:128 * hd + 128], start=(first and hd % 4 == 0), stop=False))
                    fns.append(I('matmul',
                        PS(6)[0:8, hd:hd + 1], lhsT=pe_[:, 8 * hd:8 * hd + 8], rhs=onesb[:, 0:1], start=(first and hd == 0), stop=False))
                P.op('pe', fns, reads=[('R5', 'pe', slot), ('R3', 'vst', slot), ('c', 'ones')], writes=[pk(4), pk(5), pk(6)])
                first = False
            for g in range(3):
                slot = cnt % 2
                cnt += 1
                sbk = 2 + slot
                fns = [I('matmul', PS(sbk)[0:32, 8 * hd:8 * hd + 8], lhsT=KTS1[:, g, hd, :],
                                                                    rhs=QTS1[:, g, hd, 8 * s:8 * s + 8], start=True, stop=True)
                       for hd in range(8)]
                P.op('pe', fns, reads=[('c', 'kTs1', g, hd) for hd in range(8)] + [('c', 'qTs1', g, hd) for hd in range(8)],
                     writes=[pk(sbk)])
                pe_ = PE_[slot]
                P.op('act', I('activation', out=pe_[0:32, :], in_=PS(sbk)[0:32, 0:64], func=AF.Exp, scale=SC_C),
                     reads=[pk(sbk)], writes=[('R5', 'pe', slot)])
                mcol = 8 * (4 * g + s)
                P.op('dve', I('tensor_tensor',
                    out=pe_[0:32, :].rearrange("p (a t) -> p a t", a=8), in0=pe_[0:32, :].rearrange("p (a t) -> p a t", a=8),
                    in1=cmaskn[0:32, mcol:mcol + 8].unsqueeze(1).to_broadcast([32, 8, 8]), op=ALU.mult),
                    reads=[('R5', 'pe', slot), ('c', 'cmaskn')], writes=[('R5', 'pe', slot)])
                fns = []
                lastg = (g == 2)
                for hd in range(8):
                    fns.append(I('matmul',
                        PS(4 + hd // 4)[0:8, 128 * (hd % 4):128 * (hd % 4) + 128], lhsT=pe_[0:32, 8 * hd:8 * hd + 8],
                        rhs=VS1[0:32, g, hd, :], start=False, stop=lastg))
                    fns.append(I('matmul',
                        PS(6)[0:8, hd:hd + 1], lhsT=pe_[0:32, 8 * hd:8 * hd + 8], rhs=onesb[0:32, 0:1], start=False, stop=lastg))
                P.op('pe', fns, reads=[('R5', 'pe', slot), ('c', 'ones')] + [('c', 'Vs1', g, hd) for hd in range(8)],
                     writes=[pk(4), pk(5), pk(6)])
            P.op('dve', I('reciprocal', out=rD[0:8, :], in_=PS(6)[0:8, 0:8]), reads=[pk(6)], writes=[('R5', 'rD')])
            for hf in range(2):
                P.op('dve', I('tensor_tensor',
                    out=o32s[0:8, 512 * hf:512 * hf + 512].rearrange("p (a d) -> p a d", a=4),
                    in0=PS(4 + hf)[0:8, :].rearrange("p (a d) -> p a d", a=4),
                    in1=rD[0:8, 4 * hf:4 * hf + 4].unsqueeze(2).to_broadcast([8, 4, 128]), op=ALU.mult),
                    reads=[pk(4 + hf), ('R5', 'rD')], writes=[('R5', 'o32s', hf)])
            fns = [I('transpose', out=PS(7)[:, 8 * hd:8 * hd + 8], in_=o32s[0:8, 128 * hd:128 * hd + 128],
                                                identity=identf[0:8, 0:8]) for hd in range(8)]
            P.op('pe', fns, reads=[('R5', 'o32s', 0), ('R5', 'o32s', 1), ('c', 'identf')], writes=[pk(7)])
            P.op('dve', I('tensor_tensor', out=MIX1[:, :, 2048 + 8 * s:2048 + 8 * s + 8],
                                                       in0=PS(7)[:, 0:64].rearrange("p (a t) -> p a t", a=8),
                                                       in1=GTS[:, :, 8 * s:8 * s + 8], op=ALU.mult),
                 reads=[pk(7)] + [('c', 'gTs', hd) for hd in range(8)], writes=[('mix1', 16)])

        _phase_end('sampleC')
        P.fence(['R3', 'R5', 'R7'])
        for cc in range(8):
            P.dma('pool', I('dma_start', out=WO1[:, cc, :], in_=wo1[:, cc, :]), ('wo', cc % 4), writes=[('wo', 'w')])
        outproj_layer(1, MIX1, 8, WO1, ('wo', 'w'),
                      (lambda t, rows, nb: x1d[t * 128:t * 128 + rows, nb * 512:(nb + 1) * 512]), False)

        P.dead = False
        P.wait('sp', *P.all_tokens())
        P.emit()
    return nc


def _slabify(w, colsets):
    out = np.empty((len(colsets), 128, 16, 256), np.float32)
    w3 = w.reshape(16, 128, -1)
    for i, cols in enumerate(colsets):
        out[i] = np.transpose(w3[:, :, cols], (1, 0, 2))
    return out


def _rope_table(dh, positions):
    half = dh // 2
    inv_freq = (np.float32(10000.0) ** (-(np.arange(0, dh, 2, dtype=np.float32) / np.float32(dh)))).astype(np.float32)
    ang = positions.astype(np.float32)[:, None] * inv_freq[None, :]
    return np.cos(ang).astype(np.float32), np.sin(ang).astype(np.float32)


def _consts():
    c = {}
    pos = np.zeros((17, 128), np.float32)
    for t in range(16):
        pos[t] = 128 * t + np.arange(128)
    pos[16, :32] = 8192 + (np.arange(32) % 8)
    for name, dh in (("rope0", 64), ("rope1", 128)):
        half = dh // 2
        cs, sn = _rope_table(dh, pos.reshape(-1))
        tab = np.concatenate([cs.reshape(17, 128, half), sn.reshape(17, 128, half)], axis=2)
        c[name] = np.ascontiguousarray(np.transpose(tab, (1, 0, 2)).reshape(128, 17 * dh))
    j = np.arange(128)[:, None]
    i = np.arange(128)[None, :]
    c["maskul"] = np.concatenate([(j <= i), (j >= i)], axis=1).astype(np.float32)
    c["ident"] = np.eye(128, dtype=np.float32)
    sm8 = np.zeros((16, 8), np.float32)
    c1 = np.zeros((16, 8), np.float32)
    c2 = np.zeros((16, 8), np.float32)
    for m in range(2):
        for tq in range(8):
            sm8[m * 8 + tq, :tq + 1] = 1.0
            (c1 if m == 0 else c2)[m * 8 + tq, tq] = 1.0
    c["smask8"] = sm8
    c["c12s"] = np.concatenate([c1, c2], axis=1)
    cs, sn = _rope_table(64, 8192 + np.arange(8))
    c["rope_s"] = np.concatenate([cs, sn], axis=1).astype(np.float32)
    cm = np.zeros((13, 128, 8), np.float32)
    mm = np.arange(128)
    for t in range(8):
        cm[0, :, t] = (mm >= t)
    for r in range(4):
        for t in range(8):
            if t % 4 == r:
                cm[1 + r, :, t] = (mm >= (t - r) // 4)
    for r in range(8):
        cm[5 + r, :, r] = 1.0
    c["cmask"] = np.ascontiguousarray(np.transpose(cm, (1, 0, 2)).reshape(128, 104))
    cn = np.zeros((32, 3, 4, 8), np.float32)
    for s in range(4):
        for tp in range(8):
            for t in range(8):
                if tp <= t:
                    cn[s * 8 + tp, 0, s, t] = 1.0
                if tp == t or tp == t - 4:
                    cn[s * 8 + tp, 1, s, t] = 1.0
                if tp == t:
                    cn[s * 8 + tp, 2, s, t] = 1.0
    c["cmaskn"] = cn.reshape(32, 96)
    return c


def _layer0_colsets():
    sets = []
    for c in range(8):
        sets.append(np.concatenate([np.arange(c * 128, c * 128 + 128), 1024 + np.arange(c * 128, c * 128 + 128)]))
    for i in range(4):
        sets.append(2048 + np.arange(256 * i, 256 * i + 256))
    for hp in range(4):
        for blk in (3, 4, 5, 6):
            sets.append(1024 * blk + np.arange(256 * hp, 256 * hp + 256))
    return sets


def _layer1_colsets():
    sets = []
    for h in range(8):
        hc = np.arange(128 * h, 128 * h + 128)
        for g in range(3):
            sets.append(np.concatenate([1024 * (3 * g) + hc, 1024 * (3 * g + 1) + hc]))
        sets.append(np.concatenate([1024 * 2 + hc, 1024 * 5 + hc]))
        sets.append(np.concatenate([1024 * 8 + hc, 1024 * 9 + hc]))
    return sets


_NC_CACHE = {}
_TEST_CORES = None


def _run_test(in_maps):
    nc = build_program(ncores=1)
    res = run_bass_kernel_spmd(nc, [in_maps[0]], core_ids=[0])
    return res.results[0]


def kernel(x_prompt, x_sample, cache_kb, cache_vb, state_conv, state_kv_c0, state_kv_c1, state_kv_c2,
           page_table, w_in_ab, b_in_ab, w_dw, b_dw, ln_a_g, ln_a_b, lam_q1, lam_k1, lam_q2, lam_k2,
           subln_g, w_out_ab, w_in_c, b_in_c, w_out_c, post_ln_g, post_ln_b):
    f = lambda a: np.ascontiguousarray(np.asarray(a, dtype=np.float32))
    x_prompt = f(x_prompt); x_sample = f(x_sample)
    cs0 = _layer0_colsets()
    cs1 = _layer1_colsets()
    w_in_ab = f(w_in_ab); w_in_c = f(w_in_c)
    b0 = f(b_in_ab)[0]; b1 = f(b_in_c)[0]
    shared = dict(_consts())
    shared["w0"] = _slabify(w_in_ab[0], cs0)
    shared["w1"] = _slabify(w_in_c[0], cs1)
    shared["btm0"] = np.stack([b0[c] for c in cs0]).astype(np.float32)
    shared["btm1"] = np.stack([b1[c] for c in cs1]).astype(np.float32)
    shared["bfm0"] = np.ascontiguousarray(np.stack([b0[c].reshape(2, 128).T for c in cs0], axis=1).reshape(128, 56))
    shared["bfm1"] = np.ascontiguousarray(np.stack([b1[c].reshape(2, 128).T for c in cs1], axis=1).reshape(128, 80))
    shared["wo0"] = np.ascontiguousarray(np.transpose(f(w_out_ab)[0].reshape(16, 128, 2048), (1, 0, 2)))
    shared["wo1"] = np.ascontiguousarray(np.transpose(f(w_out_c)[0].reshape(8, 128, 2048), (1, 0, 2)))
    shared["wdw"] = np.ascontiguousarray(np.transpose(f(w_dw)[0].reshape(31, 8, 128), (2, 1, 0)).reshape(128, 248))
    cp = np.stack([f(b_dw)[0].reshape(8, 128).T, f(ln_a_g)[0].reshape(8, 128).T, f(ln_a_b)[0].reshape(8, 128).T], axis=1)
    shared["cpar"] = np.ascontiguousarray(cp.reshape(128, 24))
    shared["lamv"] = np.concatenate([f(lam_q1)[0], f(lam_k1)[0], f(lam_q2)[0], f(lam_k2)[0]]).reshape(1, 256)
    shared["subg"] = f(subln_g)[0].reshape(128, 1)
    shared["pln"] = np.stack([f(post_ln_g)[0], f(post_ln_b)[0], f(post_ln_g)[1], f(post_ln_b)[1]])
    ckb = f(cache_kb)[0]
    cvb = f(cache_vb)[0]
    shared["xsa"] = x_sample.reshape(256, D)
    shared["pt"] = np.ascontiguousarray(np.asarray(page_table).astype(np.int32).reshape(1, 32 * NPAGES))
    w_ab = w_in_ab[0]
    sc = f(state_conv)[0]
    skvs = [f(state_kv_c0)[0], f(state_kv_c1)[0], f(state_kv_c2)[0]]
    pt = np.asarray(page_table).astype(np.int32)

    in_maps = []
    for c in range(NCORES):
        m = dict(shared)
        m["xp"] = x_prompt[c % 4]
        m["xs"] = np.ascontiguousarray(x_sample[4 * c:4 * c + 4].reshape(32, D))
        hc = np.arange(128 * c, 128 * c + 128)
        cols = np.concatenate([3072 + hc, 4096 + hc, 5120 + hc, 6144 + hc])
        m["wsb"] = np.ascontiguousarray(np.transpose(w_ab[:, cols].reshape(16, 128, 512), (1, 0, 2)))
        m["bsb_tm"] = np.ascontiguousarray(b0[cols[0:384]].reshape(1, 384))
        m["bsb_fm"] = np.ascontiguousarray(b0[cols[384:512]].reshape(128, 1))
        m["kbh"] = np.ascontiguousarray(ckb[:, :, 2 * c:2 * c + 2, :]).reshape(NPHYS * 128, 128)
        m["vbh"] = np.ascontiguousarray(cvb[:, :, c, :]).reshape(NPHYS * 128, 128)
        sel = np.zeros((256, 32), np.float32)
        sel[32 * c + np.arange(32), np.arange(32)] = 1.0
        m["sel"] = np.ascontiguousarray(np.transpose(sel.reshape(2, 128, 32), (1, 0, 2)).reshape(128, 64))
        m["sconvT"] = np.ascontiguousarray(np.transpose(sc[4 * c:4 * c + 4], (0, 2, 1)))
        for g in range(3):
            m["skv%d" % g] = np.ascontiguousarray(skvs[g][4 * c:4 * c + 4].reshape(4, C_W[g], 2, 1024))
        in_maps.append(m)

    if _TEST_CORES is not None:
        return _run_test(in_maps)
    if "nc" not in _NC_CACHE:
        _NC_CACHE["nc"] = build_program()
    nc = _NC_CACHE["nc"]
    res = run_bass_kernel_spmd(nc, in_maps, core_ids=list(range(NCORES)))
    R = res.results

    y_prompt = np.stack([R[b]["y_p"] for b in range(4)])
    y_sample = np.concatenate([R[c]["y_s"].reshape(4, 8, D) for c in range(NCORES)])

    def kbfix(a, rows):
        return np.transpose(a.reshape(4, rows, 4, 64), (1, 0, 2, 3)).reshape(rows, 16, 64)

    def vbfix(a, rows):
        return np.transpose(a.reshape(4, rows, 2, 128), (1, 0, 2, 3)).reshape(rows, 8, 128)

    kb_prompt = np.stack([kbfix(R[b]["kb_p"], T) for b in range(4)])[None]
    vb_prompt = np.stack([vbfix(R[b]["vb_p"], T) for b in range(4)])[None]
    conv_prompt = np.stack([R[b]["conv_pT"].T for b in range(4)])[None]
    kb_sample = np.concatenate([R[c]["kb_sh"].reshape(32, 8, 2, 64) for c in range(NCORES)], axis=2)[None]
    vb_sample = np.stack([R[c]["vb_sh"].reshape(32, 8, 128) for c in range(NCORES)], axis=2)[None]
    conv_sample = np.concatenate([np.transpose(R[c]["conv_sT"], (0, 2, 1)) for c in range(NCORES)])[None]
    kvc_p = []
    for g in range(3):
        W = min(C_W[g], T)
        kvc_p.append(np.stack([np.transpose(R[b]["kvc%d_p" % g], (2, 0, 1, 3)) for b in range(4)])[None])
    kvc_s = []
    for g in range(3):
        kvc_s.append(np.concatenate([R[c]["kvc%d_s" % g].reshape(4, C_W[g], 2, 8, 128) for c in range(NCORES)])[None])
    outs = (y_prompt, y_sample, kb_prompt, vb_prompt, conv_prompt, kb_sample, vb_sample, conv_sample,
            kvc_p[0], kvc_p[1], kvc_p[2], kvc_s[0], kvc_s[1], kvc_s[2])
    return tuple(np.ascontiguousarray(o, dtype=np.float32) for o in outs)
```
